# Optimizing a Trainium2 kernel written in Bass

```python
import math
import jax, jax.numpy as jnp
from jax import lax
import numpy as np

D_MODEL = 1024
BATCH = 8
SEQ = 2048
DEPTH = 1

MEM_LEN = 256
EPS = 1e-6
DA_HEADS = 4
DA_QK_DIM = 64
DA_V_DIM = 2 * DA_QK_DIM
DA_WIDTH = DA_HEADS * DA_V_DIM
QK_WIDTH = DA_HEADS * 2 * DA_QK_DIM
POOL_WINDOWS = (2, 4, 8, 16)
POOL_GROUPS = len(POOL_WINDOWS)
POOL_WIDTH = D_MODEL - DA_WIDTH
POOL_GROUP_DIM = POOL_WIDTH // POOL_GROUPS
MIX_WIDTH = DA_WIDTH + POOL_WIDTH
IN_WIDTH = 2 * QK_WIDTH + DA_WIDTH + POOL_WIDTH
ROPE_THETA = 500000.0
ROPE_DIM = DA_QK_DIM // 4
BLOCK_Q = 128
X_HEADS = 4
X_HEAD_DIM = D_MODEL // X_HEADS
D_FF = -(-(8 * D_MODEL) // (3 * 256)) * 256
def _lambda_init(layer_idx):
    return 0.8 - 0.6 * math.exp(-0.3 * (layer_idx - 1))

kernel_name = "hybrid_diffattn_pool_memxattn_swiglu"


def rmsnorm(x, g):
    xf = x.astype(jnp.float32)
    y = xf * lax.rsqrt(jnp.mean(xf * xf, axis=-1, keepdims=True) + EPS)
    return (y * g.astype(jnp.float32)).astype(x.dtype)


def rope_tables(positions, dtype):
    inv_freq = ROPE_THETA ** (-jnp.arange(0, ROPE_DIM, 2, dtype=jnp.float32) / ROPE_DIM)
    ang = positions.astype(jnp.float32)[..., None] * inv_freq
    return jnp.cos(ang).astype(dtype), jnp.sin(ang).astype(dtype)


def apply_partial_rope(t, cos, sin):
    cos = cos[:, :, None, None, :]
    sin = sin[:, :, None, None, :]
    half = ROPE_DIM // 2
    t1 = t[..., :half]
    t2 = t[..., half:ROPE_DIM]
    rest = t[..., ROPE_DIM:]
    return jnp.concatenate([t1 * cos - t2 * sin, t2 * cos + t1 * sin, rest], axis=-1)


def diff_attention(q, k, v, lam):
    seq = q.shape[3]
    scale = DA_QK_DIM ** -0.5
    outs = []
    for i in range(seq // BLOCK_Q):
        q0 = i * BLOCK_Q
        kend = q0 + BLOCK_Q
        qb = q[:, :, :, q0:kend]
        kb = k[:, :, :, :kend]
        vb = v[:, :, :kend]
        s = jnp.einsum('bhmqd,bhmkd->bhmqk', qb, kb).astype(jnp.float32) * scale
        mask = (q0 + jnp.arange(BLOCK_Q))[:, None] >= jnp.arange(kend)[None, :]
        s = jnp.where(mask, s, -jnp.inf)
        p = jax.nn.softmax(s, axis=-1)
        a = p[:, :, 0] - lam * p[:, :, 1]
        outs.append(jnp.einsum('bhqk,bhkd->bhqd', a.astype(v.dtype), vb))
    return jnp.concatenate(outs, axis=2)


def causal_multiscale_pool(u):
    b, s, _ = u.shape
    uf = u.astype(jnp.float32).reshape(b, s, POOL_GROUPS, POOL_GROUP_DIM)
    c = jnp.cumsum(uf, axis=1)
    t1 = jnp.arange(1, s + 1, dtype=jnp.float32)
    groups = []
    for g, w in enumerate(POOL_WINDOWS):
        cg = c[:, :, g]
        shifted = jnp.pad(cg, ((0, 0), (w, 0), (0, 0)))[:, :s]
        count = jnp.minimum(t1, float(w))[None, :, None]
        groups.append((cg - shifted) / count - uf[:, :, g])
    return jnp.stack(groups, axis=2).astype(u.dtype)


def setup_inputs(seed: int = 0) -> dict:
    key = jax.random.key(seed)
    ks = jax.random.split(key, 32)
    f32 = jnp.float32

    def w(k, shape, fan_in):
        return jax.random.normal(k, shape, f32) * (fan_in ** -0.5)

    def gain(k, n):
        return 1.0 + 0.02 * jax.random.normal(k, (n,), f32)

    return {
        "x": jax.random.normal(ks[0], (BATCH, SEQ, D_MODEL), f32),
        "mem": jax.random.normal(ks[1], (BATCH, MEM_LEN, D_MODEL), f32),
        "positions": jnp.broadcast_to(jnp.arange(SEQ, dtype=jnp.int32), (BATCH, SEQ)),
        "g_mix_pre": gain(ks[2], D_MODEL),
        "w_in": w(ks[3], (D_MODEL, IN_WIDTH), D_MODEL),
        "lambda_q1": 0.1 * jax.random.normal(ks[4], (DA_QK_DIM,), f32),
        "lambda_k1": 0.1 * jax.random.normal(ks[5], (DA_QK_DIM,), f32),
        "lambda_q2": 0.1 * jax.random.normal(ks[6], (DA_QK_DIM,), f32),
        "lambda_k2": 0.1 * jax.random.normal(ks[7], (DA_QK_DIM,), f32),
        "g_subln": gain(ks[8], DA_V_DIM),
        "w_pool": w(ks[9], (POOL_GROUPS, POOL_GROUP_DIM, POOL_GROUP_DIM), POOL_GROUP_DIM),
        "pool_scale": 1.0 + 0.1 * jax.random.normal(ks[10], (POOL_WIDTH,), f32),
        "w_out": w(ks[11], (MIX_WIDTH, D_MODEL), MIX_WIDTH),
        "g_mix_post": gain(ks[12], D_MODEL),
        "g_x_pre": gain(ks[13], D_MODEL),
        "g_mem": gain(ks[14], D_MODEL),
        "w_xq": w(ks[15], (D_MODEL, X_HEADS * X_HEAD_DIM), D_MODEL),
        "w_xkv": w(ks[16], (D_MODEL, 2 * X_HEADS * X_HEAD_DIM), D_MODEL),
        "w_xo": w(ks[17], (X_HEADS * X_HEAD_DIM, D_MODEL), X_HEADS * X_HEAD_DIM),
        "g_x_post": gain(ks[18], D_MODEL),
        "g_ffn_pre": gain(ks[19], D_MODEL),
        "w_gate": w(ks[20], (D_MODEL, D_FF), D_MODEL),
        "w_up": w(ks[21], (D_MODEL, D_FF), D_MODEL),
        "w_down": w(ks[22], (D_FF, D_MODEL), D_FF),
        "g_ffn_post": gain(ks[23], D_MODEL),
    }


def reference(x, mem, positions, g_mix_pre, w_in, lambda_q1, lambda_k1, lambda_q2,
              lambda_k2, g_subln, w_pool, pool_scale, w_out, g_mix_post, g_x_pre,
              g_mem, w_xq, w_xkv, w_xo, g_x_post, g_ffn_pre, w_gate, w_up, w_down,
              g_ffn_post):
    b, s, _ = x.shape
    cos, sin = rope_tables(positions, x.dtype)
    mem_n = rmsnorm(mem, g_mem)

    for layer in range(DEPTH):
        lam_init = _lambda_init(layer + 1)

        h = rmsnorm(x, g_mix_pre)
        proj = h @ w_in
        q = proj[..., :QK_WIDTH].reshape(b, s, DA_HEADS, 2, DA_QK_DIM)
        k = proj[..., QK_WIDTH:2 * QK_WIDTH].reshape(b, s, DA_HEADS, 2, DA_QK_DIM)
        v = proj[..., 2 * QK_WIDTH:2 * QK_WIDTH + DA_WIDTH].reshape(b, s, DA_HEADS, DA_V_DIM)
        u = proj[..., 2 * QK_WIDTH + DA_WIDTH:]

        q = apply_partial_rope(q, cos, sin).transpose(0, 2, 3, 1, 4)
        k = apply_partial_rope(k, cos, sin).transpose(0, 2, 3, 1, 4)
        v = v.transpose(0, 2, 1, 3)
        lam = (jnp.exp(jnp.sum(lambda_q1.astype(jnp.float32) * lambda_k1.astype(jnp.float32)))
               - jnp.exp(jnp.sum(lambda_q2.astype(jnp.float32) * lambda_k2.astype(jnp.float32)))
               + lam_init)
        da = diff_attention(q, k, v, lam)
        da = rmsnorm(da, g_subln) * (1.0 - lam_init)
        da = da.transpose(0, 2, 1, 3).reshape(b, s, DA_WIDTH)

        pooled = causal_multiscale_pool(u)
        po = jnp.einsum('bsgc,gcd->bsgd', pooled, w_pool).reshape(b, s, POOL_WIDTH)
        po = po * pool_scale

        mix = jnp.concatenate([da, po], axis=-1) @ w_out
        x = x + rmsnorm(mix, g_mix_post)

        hq = rmsnorm(x, g_x_pre)
        xq = (hq @ w_xq).reshape(b, s, X_HEADS, X_HEAD_DIM)
        kv = (mem_n @ w_xkv).reshape(b, MEM_LEN, 2, X_HEADS, X_HEAD_DIM)
        sc = jnp.einsum('bshd,bmhd->bhsm', xq, kv[:, :, 0]).astype(jnp.float32) * (X_HEAD_DIM ** -0.5)
        pm = jax.nn.softmax(sc, axis=-1).astype(x.dtype)
        xo = jnp.einsum('bhsm,bmhd->bshd', pm, kv[:, :, 1]).reshape(b, s, X_HEADS * X_HEAD_DIM)
        x = x + rmsnorm(xo @ w_xo, g_x_post)

        hf = rmsnorm(x, g_ffn_pre)
        ff = (jax.nn.silu(hf @ w_gate) * (hf @ w_up)) @ w_down
        x = x + rmsnorm(ff, g_ffn_post)

    return x
```

```python
import math
import numpy as np
import concourse.bass as bass
import concourse.mybir as mybir
from concourse.bass_utils import run_bass_kernel_spmd

F32 = mybir.dt.float32
BF16 = mybir.dt.bfloat16
I32 = mybir.dt.int32
AF = mybir.ActivationFunctionType
ALU = mybir.AluOpType

S = 2048
D = 1024
NT = 16
KC = 8
DFF = 2816
NFC = 22
MEM = 256
EPS = 1e-6
LAM_INIT = 0.8 - 0.6 * math.exp(-0.3 * 0)
ROPE_THETA = 500000.0
TWO_PI = 2.0 * math.pi


class Op:
    __slots__ = ("idx", "eng", "fn", "deps", "dma", "sig", "dval")

    def __init__(self, idx, eng, fn, deps, dma):
        self.idx = idx
        self.eng = eng
        self.fn = fn
        self.deps = deps
        self.dma = dma
        self.sig = None
        self.dval = None


class Sched:
    ENGS = ("pe", "act", "dve", "pool", "sp")

    def __init__(self):
        self.ops = []
        self.lastw = {}
        self.readers = {}
        self.group_of = {}
        self.final_slots = []

    def add(self, eng, fn, reads=(), writes=(), dma=None):
        idx = len(self.ops)
        rk = [k for k in reads if k[0] != "ps"]
        wk = list(writes) + [k for k in reads if k[0] == "ps" and k not in writes]
        for k in list(rk) + list(wk):
            g = self.group_of.get(k[0])
            if g is not None and (g,) not in rk:
                rk.append((g,))
        deps = set()
        for k in rk:
            w = self.lastw.get(k)
            if w is not None:
                deps.add(w)
        for k in wk:
            w = self.lastw.get(k)
            if w is not None:
                deps.add(w)
            r = self.readers.get(k)
            if r:
                deps.update(r.values())
        deps.discard(idx)
        rkey = ("d", idx) if dma is not None else eng
        for k in rk:
            self.readers.setdefault(k, {})[rkey] = idx
        for k in wk:
            self.lastw[k] = idx
            self.readers[k] = {}
        op = Op(idx, eng, fn, deps, dma)
        self.ops.append(op)
        return op

    def emit(self, nc, block, eng_sem, dma_sem):
        ops = self.ops
        need_sig = set()
        for op in ops:
            for d in op.deps:
                dop = ops[d]
                if dop.dma is not None:
                    continue
                if dop.eng == "pe" and op.eng == "pe" and op.dma is None:
                    continue
                need_sig.add(d)
        cnt = {e: 0 for e in self.ENGS}
        tot = {}
        for op in ops:
            if op.dma is not None:
                tot[op.dma] = tot.get(op.dma, 0) + 16
                op.dval = tot[op.dma]
            elif op.idx in need_sig:
                cnt[op.eng] += 1
                op.sig = cnt[op.eng]
        self.stats = dict(cnt)
        by_eng = {e: [op for op in ops if op.eng == e] for e in self.ENGS}

        def body_for(eng):
            def body(e):
                waited = {}
                for op in by_eng[eng]:
                    need = {}
                    for d in op.deps:
                        dop = ops[d]
                        if dop.dma is not None:
                            key = ("d", dop.dma)
                            val = dop.dval
                        else:
                            if dop.eng == "pe" and eng == "pe" and op.dma is None:
                                continue
                            key = ("c", dop.eng)
                            val = dop.sig
                        if val > need.get(key, 0):
                            need[key] = val
                    for key, val in need.items():
                        if waited.get(key, 0) >= val:
                            continue
                        waited[key] = val
                        sem = dma_sem[key[1]] if key[0] == "d" else eng_sem[key[1]]
                        e.wait_ge(sem, val)
                    ins = op.fn(e)
                    if op.dma is not None:
                        ins.then_inc(dma_sem[op.dma], 16)
                    elif op.idx in need_sig:
                        ins.then_inc(eng_sem[eng], 1)
                if eng == "sp":
                    for slot in self.final_slots:
                        e.wait_ge(dma_sem[slot], tot[slot])
            return body

        block.tensor(body_for("pe"))
        block.scalar(body_for("act"))
        block.vector(body_for("dve"))
        block.gpsimd(body_for("pool"))
        block.sync(body_for("sp"))


class Arena:
    def __init__(self, t, nwords):
        self.t = t
        self.n = nwords
        self.off = 0

    def reset(self):
        self.off = 0

    def alloc(self, shape, dtype):
        n = 1
        for s in shape:
            n *= s
        esz = 4 if dtype in (F32, I32) else 2
        words = (n * esz + 3) // 4
        assert self.off + words <= self.n, (self.off, words, self.n)
        ap = self.t[:, self.off:self.off + words]
        self.off += words
        if dtype != F32:
            ap = ap.bitcast(dtype)
        ap = ap[:, 0:n]
        if len(shape) == 2:
            return ap.rearrange("p (a b) -> p a b", a=shape[0])
        if len(shape) == 3:
            return ap.rearrange("p (a b c) -> p a b c", a=shape[0], b=shape[1])
        return ap


def build_nc(upto="all", dbg_words=0):
    nc = bass.Bass("TRN2", target_bir_lowering=False)

    def din(name, shape, dt=F32):
        return nc.dram_tensor(name, list(shape), dt, kind="ExternalInput").ap()

    x_d = din("x", [S, D])
    mem_d = din("mem", [MEM, D])
    pos_d = din("pos", [128, NT], I32)
    g_d = {n: din(n, [1, D]) for n in ("g_mix_pre", "g_mix_post", "g_x_pre", "g_mem", "g_x_post", "g_ffn_pre", "g_ffn_post")}
    lam_d = din("lamv", [1, 256])
    gsub_d = din("g_subln", [1, 128])
    wpool_d = din("w_pool", [128, 4, 128])
    pscale_d = din("pool_scale", [128, 4])
    w_in_d = din("w_in", [D, 2048])
    w_out_d = din("w_out", [D, D])
    w_xq_d = din("w_xq", [D, D])
    w_xkv_d = din("w_xkv", [D, 2048])
    w_xo_d = din("w_xo", [D, D])
    w_gate_d = din("w_gate", [D, DFF])
    w_up_d = din("w_up", [D, DFF])
    w_down_d = din("w_down", [DFF, D])
    out_d = nc.dram_tensor("out", [S, D], F32, kind="ExternalOutput").ap()
    wg_s = nc.dram_tensor("wg_s", [11, 128, KC * 256], BF16, kind="Internal").ap()
    wu_s = nc.dram_tensor("wu_s", [11, 128, KC * 256], BF16, kind="Internal").ap()
    wd_s = nc.dram_tensor("wd_s", [128, NFC * D], BF16, kind="Internal").ap()
    dbg_d = None
    if dbg_words:
        dbg_d = nc.dram_tensor("dbg", [128, dbg_words], F32, kind="ExternalOutput").ap()

    R2W = 25100
    from contextlib import ExitStack
    with ExitStack() as es:
        def sb(name, shape, dt):
            return es.enter_context(nc.sbuf_tensor(name, list(shape), dt))

        xa_t = sb("xa", [128, 16384], F32)
        actT = sb("actT", [128, KC, S], BF16)
        r2_t = sb("r2", [128, R2W], F32)
        ident = sb("ident", [128, 128], BF16)
        cc = sb("cc", [128, NT, 16], F32)
        ns = sb("ns", [128, NT, 16], F32)
        gb = sb("gb", [128, D], F32)
        gsub8 = sb("gsub8", [128, 128], F32)
        rc = sb("rc", [128, 4, 16], F32)
        pscale = sb("pscale", [128, 4], F32)
        htmp = sb("htmp", [128, 2, D], BF16)
        small = sb("small", [128, 224], F32)
        posi = sb("posi", [128, NT], I32)
        ki = sb("ki", [128, NT * 16], I32)
        ps = es.enter_context(nc.psum_tensor("ps", [128, 8, 512], F32))

        sm_off = [0]

        def sm(n):
            a = small[:, sm_off[0]:sm_off[0] + n]
            sm_off[0] += n
            assert sm_off[0] <= 224
            return a

        warm = sm(2)
        neghalf = sm(1)
        invt = sm(16)
        neg_lam = sm(1)
        lam_t = sm(4)
        ss_n = sm(16)
        vv_n = sm(16)
        rstd_n = sm(16)
        ss_p = sm(16)
        vv_p = sm(16)
        rstd_p = sm(16)
        ss_d = sm(4)
        vv_d = sm(4)
        rstd_d = sm(4)
        rr9 = sm(9)
        rr9b = sm(9)
        ss_d2 = sm(4)
        vv_d2 = sm(4)
        rstd_d2 = sm(4)
        rr2 = sm(4)
        posf = sm(16)
        t16 = sm(16)
        iot = sm(16)

        X = xa_t[:, :].rearrange("p (a b) -> p a b", a=NT)
        XA = Arena(xa_t, 16384)
        R2 = Arena(r2_t, R2W)

        def psb(bank):
            return ps[:, bank, :].bitcast(BF16).rearrange("p (a b) -> p a b", a=8)

        sch = Sched()
        sch.group_of.update({})
        sems = {}
        dma_slots = []

        def slot(name):
            if name not in dma_slots:
                dma_slots.append(name)
            return name

        bank_rr = [0]

        def next_bank():
            b = bank_rr[0]
            bank_rr[0] = (b + 1) % 8
            return b

        def next_pair():
            b = bank_rr[0]
            if b % 2:
                b = (b + 1) % 8
            bank_rr[0] = (b + 2) % 8
            return b

        ev_rr = [0]

        def alt_eng():
            ev_rr[0] ^= 1
            return "act" if ev_rr[0] else "dve"

        def K(name, *idx):
            return (name,) + tuple(idx)

        def actT_keys(kcs, tts):
            return [("actT", kc, tt) for kc in kcs for tt in tts]

        def mm(out_ap, pairs, reads, writes, first_start=True, last_stop=True, skip=False):
            def fn(e):
                ins = None
                n = len(pairs)
                for i, (l, r) in enumerate(pairs):
                    ins = e.matmul(out_ap, l, r, start=(first_start and i == 0), stop=(last_stop and i == n - 1),
                                   skip_group_check=skip)
                return ins
            return sch.add("pe", fn, reads=reads, writes=writes)

        def evac(eng, out_ap, in_ap, reads, writes, scale=None):
            if eng == "act":
                if scale is None:
                    fn = lambda e: e.activation(out=out_ap, in_=in_ap, func=AF.Copy)
                else:
                    fn = lambda e: e.activation(out=out_ap, in_=in_ap, func=AF.Copy, scale=scale)
            else:
                assert scale is None
                fn = lambda e: e.tensor_copy(out=out_ap, in_=in_ap)
            return sch.add(eng, fn, reads=reads, writes=writes)

        def _dma_load(queue, slotname, out_ap, in_ap, writes, reads=()):
            return dma_load(queue, slotname, out_ap, in_ap, writes, reads)

        def dma_load(queue, slotname, out_ap, in_ap, writes, reads=()):
            slot(slotname)
            return sch.add(queue, lambda e: e.dma_start(out=out_ap, in_=in_ap), reads=reads, writes=writes, dma=slotname)

        def load_gb(name):
            dma_load("sp", "gb", gb[:, :], g_d[name][0:1, :].to_broadcast([128, D]), writes=[K("gb")])

        def rstd_chain(ss_ap, vv_ap, rstd_ap, n, key):
            sch.add("pool", lambda e: e.tensor_scalar(out=vv_ap, in0=ss_ap, scalar1=1.0 / n, scalar2=EPS, op0=ALU.mult, op1=ALU.add),
                    reads=[K("ss", key)], writes=[K("vv", key)])
            nh = neghalf if vv_ap.shape[1] == 1 else neghalf.to_broadcast([128, vv_ap.shape[1]])
            sch.add("pool", lambda e: e.tensor_tensor(out=rstd_ap, in0=vv_ap, in1=nh, op=ALU.pow),
                    reads=[K("vv", key), K("neghalf")], writes=[K("rstd", key)])

        bar = sm(1)

        def phase_barrier():
            sch.add("pool", lambda e: e.memset(bar, 0.0), writes=[K("R2")])

        for nm in ("poT", "wsl", "xin", "U", "T", "pl", "qr", "rt1", "rt2", "wpool", "ytmp", "lamb", "xqT", "wsl2", "KcT", "Vc", "memT",
                   "mem_in", "PT2", "xo_tm", "ytmp2", "wd", "hidT", "gsl", "usl", "sg", "ytmp3", "maskc", "accS"):
            sch.group_of[nm] = "R2"

        class Pipe:
            def __init__(self, depth):
                self.depth = depth
                self.q = []

            def push(self, fn):
                self.q.append(fn)
                if len(self.q) > self.depth:
                    self.q.pop(0)()

            def flush(self):
                while self.q:
                    self.q.pop(0)()

        def norm_stats(tt, src_ap, src_keys, junk, junkk):
            sk = ("n", tt)
            ssa, vva, rsa = ss_n[:, tt:tt + 1], vv_n[:, tt:tt + 1], rstd_n[:, tt:tt + 1]
            sch.add("act", lambda e: e.activation(out=junk, in_=src_ap, func=AF.Square, accum_out=ssa),
                    reads=src_keys, writes=[junkk, K("ss", sk)])
            rstd_chain(ssa, vva, rsa, D, sk)

        def norm_scale(tt, src_ap, src_keys, dst_ap, dst_keys, ev=None, gain=None, gkeys=None):
            hs = htmp[:, tt % 2, :]
            hk = K("htmp", tt % 2)
            sk = ("n", tt)
            rsa = rstd_n[:, tt:tt + 1]
            g_ap = gb[:, :] if gain is None else gain
            g_keys = [K("gb")] if gkeys is None else list(gkeys)
            sch.add("dve", lambda e: e.scalar_tensor_tensor(out=hs, in0=src_ap, scalar=rsa, in1=g_ap, op0=ALU.mult, op1=ALU.mult),
                    reads=list(src_keys) + [K("rstd", sk)] + g_keys, writes=[hk])

            def back():
                bank = next_bank()
                pb = psb(bank)

                def tfn(e):
                    ins = None
                    for kc in range(KC):
                        ins = e.transpose(out=pb[:, kc, :], in_=hs[:, kc * 128:(kc + 1) * 128], identity=ident[:, :])
                    return ins
                sch.add("pe", tfn, reads=[hk, K("ident")], writes=[K("ps", bank)])
                evac(ev if ev is not None else alt_eng(), dst_ap, pb[:, :, :], reads=[K("ps", bank)], writes=dst_keys)
            return back

        def norm_T(tag, tt, src_ap, src_keys, dst_ap, dst_keys, junk=None, junkk=None, ev=None, gain=None, gkeys=None):
            norm_stats(tt, src_ap, src_keys, junk, junkk)
            return norm_scale(tt, src_ap, src_keys, dst_ap, dst_keys, ev=ev, gain=gain, gkeys=gkeys)

        def norm_phase(jbufs, jkey):
            for tt in range(NT):
                norm_stats(tt, X[:, tt, :], [K("X", tt)], jbufs[tt % 2], K(jkey, tt % 2))
            pipe = Pipe(1)
            for tt in range(NT):
                pipe.push(norm_scale(tt, X[:, tt, :], [K("X", tt)], actT[:, :, tt * 128:(tt + 1) * 128], actT_keys(range(KC), [tt]), ev="act"))
            pipe.flush()

        pn_cnt = [0]

        def post_norm_res(tag, tt, b0, resid_ap, resid_keys, out_ap, out_keys, ytmp_ap, ytk, after=None):
            sk = ("p", tt)
            y = ps[:, b0:b0 + 2, :]
            yk = [K("ps", b0), K("ps", b0 + 1)]
            ssa, vva, rsa = ss_p[:, tt:tt + 1], vv_p[:, tt:tt + 1], rstd_p[:, tt:tt + 1]
            yt3 = ytmp_ap.rearrange("p (a b) -> p a b", a=2)
            hs = htmp[:, tt % 2, :].rearrange("p (a b) -> p a b", a=2)
            sch.add("act", lambda e: e.activation(out=hs, in_=y, func=AF.Square, accum_out=ssa),
                    reads=yk, writes=[K("htmp", tt % 2), K("ss", sk)])
            rstd_chain(ssa, vva, rsa, D, sk)
            gb3 = gb[:, :].rearrange("p (a b) -> p a b", a=2)
            sch.add("dve", lambda e: e.scalar_tensor_tensor(out=yt3, in0=y, scalar=rsa, in1=gb3, op0=ALU.mult, op1=ALU.mult),
                    reads=yk + [K("rstd", sk), K("gb")], writes=[ytk])
            pn_cnt[0] += 1
            aeng = "dve" if (pn_cnt[0] % 2 or (tag == "p3" and tt == NT - 1) or (tag == "p1" and tt >= NT - 2)) else "pool"

            def back():
                sch.add(aeng, lambda e: e.tensor_tensor(out=out_ap, in0=ytmp_ap, in1=resid_ap, op=ALU.add),
                        reads=[ytk] + list(resid_keys), writes=out_keys)
                if after is not None:
                    after()
            return back

        def w_cols(w_ap, c0, c1):
            return w_ap.rearrange("(kc p) n -> p kc n", p=128)[:, :, c0:c1]

        def load_wslab(slotname, dst_ap, w_ap, c0, c1, key):
            dma_load("pool", slotname, dst_ap, w_cols(w_ap, c0, c1), writes=[key])

        sch.add("pool", lambda e: e.memset(ident[:, :], 0.0), writes=[K("ident")])
        sch.add("pool", lambda e: e.affine_select(out=ident[:, :], in_=ident[:, :], pattern=[[-1, 128]], compare_op=ALU.not_equal,
                                                   fill=1.0, base=0, channel_multiplier=1), reads=[K("ident")], writes=[K("ident")])
        sch.add("pool", lambda e: e.memset(warm, -0.5), writes=[K("warm")])
        sch.add("pool", lambda e: e.memset(neghalf, -0.5), writes=[K("neghalf")])
        for i in range(8):
            invf = ROPE_THETA ** (-(2.0 * i) / 16.0)
            sch.add("pool", (lambda i, invf: lambda e: e.memset(invt.rearrange("p (a b) -> p a b", a=2)[:, :, i], invf))(i, invf), reads=[K("invt")], writes=[K("invt")])
        R2.reset()
        poT = R2.alloc([4, S], BF16)
        lamb = R2.alloc([1, 256], F32)[:, 0, :]

        setup_q = []

        def drain_setup(n):
            for _ in range(n):
                if setup_q:
                    setup_q.pop(0)()

        class _Q:
            @staticmethod
            def add(*a, **k):
                setup_q.append(lambda: sch.add(*a, **k))

        def setup_tables():
          sch = _Q
          dma_load = lambda *a, **k: setup_q.append(lambda: _dma_load(*a, **k))
          dma_load("sp", "posi", posi[:, :], pos_d[:, :], writes=[K("posi")])
          sch.add("dve", lambda e: e.tensor_copy(out=posf, in_=posi[:, :]), reads=[K("posi")], writes=[K("posf")])
          ang = cc
          sch.add("dve", lambda e: e.tensor_tensor(out=ns[:, :, :], in0=posf.unsqueeze(2).to_broadcast([128, NT, 16]),
                                                   in1=invt.unsqueeze(1).to_broadcast([128, NT, 16]), op=ALU.mult),
                  reads=[K("posf"), K("invt")], writes=[K("ns")])
          sch.add("dve", lambda e: e.tensor_scalar(out=ns[:, :, 8:16], in0=ns[:, :, 8:16], scalar1=math.pi / 2, scalar2=None, op0=ALU.add),
                  reads=[K("ns")], writes=[K("ns")])
          nsf = ns[:, :, :].rearrange("p a b -> p (a b)")
          ccf = cc[:, :, :].rearrange("p a b -> p (a b)")
          sch.add("dve", lambda e: e.tensor_scalar(out=ccf, in0=nsf, scalar1=1.0 / TWO_PI, scalar2=None, op0=ALU.mult), reads=[K("ns")], writes=[K("cc")])
          sch.add("dve", lambda e: e.tensor_copy(out=ki[:, :], in_=ccf), reads=[K("cc")], writes=[K("ki")])
          sch.add("dve", lambda e: e.tensor_copy(out=ccf, in_=ki[:, :]), reads=[K("ki")], writes=[K("cc")])
          C1 = 6.28125
          C2 = TWO_PI - C1
          sch.add("dve", lambda e: e.scalar_tensor_tensor(out=nsf, in0=ccf, scalar=-C1, in1=nsf, op0=ALU.mult, op1=ALU.add),
                  reads=[K("cc"), K("ns")], writes=[K("ns")])
          sch.add("dve", lambda e: e.scalar_tensor_tensor(out=nsf, in0=ccf, scalar=-C2, in1=nsf, op0=ALU.mult, op1=ALU.add),
                  reads=[K("cc"), K("ns")], writes=[K("ns")])
          sch.add("dve", lambda e: e.tensor_scalar(out=ccf, in0=nsf, scalar1=math.pi, scalar2=None, op0=ALU.is_gt), reads=[K("ns")], writes=[K("cc")])
          sch.add("dve", lambda e: e.scalar_tensor_tensor(out=nsf, in0=ccf, scalar=-TWO_PI, in1=nsf, op0=ALU.mult, op1=ALU.add),
                  reads=[K("cc"), K("ns")], writes=[K("ns")])
          sch.add("dve", lambda e: e.tensor_scalar(out=ccf, in0=nsf, scalar1=-math.pi, scalar2=None, op0=ALU.is_lt), reads=[K("ns")], writes=[K("cc")])
          sch.add("dve", lambda e: e.scalar_tensor_tensor(out=nsf, in0=ccf, scalar=TWO_PI, in1=nsf, op0=ALU.mult, op1=ALU.add),
                  reads=[K("cc"), K("ns")], writes=[K("ns")])
          sch.add("dve", lambda e: e.tensor_scalar(out=nsf, in0=nsf, scalar1=math.pi, scalar2=-math.pi, op0=ALU.min, op1=ALU.max), reads=[K("ns")], writes=[K("ns")])
          sch.add("act", lambda e: e.activation(out=ccf, in_=nsf, func=AF.Sin), reads=[K("ns")], writes=[K("cc")])
          sch.add("dve", lambda e: e.tensor_scalar(out=ns[:, :, 0:8], in0=cc[:, :, 0:8], scalar1=-1.0, scalar2=None, op0=ALU.mult), reads=[K("cc")], writes=[K("ns")])
          sch.add("dve", lambda e: e.tensor_copy(out=ns[:, :, 8:16], in_=cc[:, :, 0:8]), reads=[K("cc"), K("ns")], writes=[K("ns")])
          sch.add("dve", lambda e: e.tensor_copy(out=cc[:, :, 0:8], in_=cc[:, :, 8:16]), reads=[K("cc"), K("ns")], writes=[K("cc")])
          dma_load("sp", "lamb", lamb, lam_d[0:1, :].to_broadcast([128, 256]), writes=[K("lamb")])
          for i in range(2):
              sch.add("dve", (lambda i: lambda e: e.tensor_tensor(out=lamb[:, 128 * i:128 * i + 64], in0=lamb[:, 128 * i:128 * i + 64], in1=lamb[:, 128 * i + 64:128 * i + 128], op=ALU.mult))(i),
                      reads=[K("lamb")], writes=[K("lamb")])
              sch.add("dve", (lambda i: lambda e: e.tensor_reduce(out=lam_t[:, i:i + 1], in_=lamb[:, 128 * i:128 * i + 64], axis=mybir.AxisListType.X, op=ALU.add))(i),
                      reads=[K("lamb")], writes=[K("lam_t")])
          sch.add("act", lambda e: e.activation(out=lam_t[:, 2:4], in_=lam_t[:, 0:2], func=AF.Exp), reads=[K("lam_t")], writes=[K("lam_t")])
          sch.add("dve", lambda e: e.scalar_tensor_tensor(out=neg_lam, in0=lam_t[:, 3:4], scalar=-LAM_INIT, in1=lam_t[:, 2:3], op0=ALU.add, op1=ALU.subtract),
                  reads=[K("lam_t")], writes=[K("neg_lam")])
          dma_load("sp", "gsub", gsub8[:, :], gsub_d[0:1, :].to_broadcast([128, 128]), writes=[K("gsub")])
          sch.add("dve", lambda e: e.tensor_scalar(out=gsub8[:, :], in0=gsub8[:, :], scalar1=1.0 - LAM_INIT, scalar2=None, op0=ALU.mult), reads=[K("gsub")], writes=[K("gsub")])
          sch.add("pool", lambda e: e.iota(out=ki[:, 0:16], pattern=[[1, 16]], base=1, channel_multiplier=0), reads=[K("ki")], writes=[K("ki")])
          sch.add("dve", lambda e: e.tensor_copy(out=iot, in_=ki[:, 0:16]), reads=[K("ki")], writes=[K("iot")])
          for g in range(4):
              sch.add("dve", (lambda g: lambda e: e.tensor_scalar(out=rc[:, g, :], in0=iot, scalar1=float(2 << g), scalar2=None, op0=ALU.min))(g), reads=[K("iot")], writes=[K("rc")])
          sch.add("dve", lambda e: e.reciprocal(out=rc[:, :, :], in_=rc[:, :, :]), reads=[K("rc")], writes=[K("rc")])
          dma_load("sp", "pscale", pscale[:, :], pscale_d[:, :], writes=[K("pscale")])

        XA.reset()
        wsl = [R2.alloc([KC, 512], BF16) for _ in range(3)]
        xin = [R2.alloc([1, D], F32)[:, 0, :] for _ in range(4)]
        Ub = [R2.alloc([1, 528], F32)[:, 0, :] for _ in range(2)]
        halo = R2.alloc([4, 16], F32)
        Tb = {"pool": [R2.alloc([1, 528], F32)[:, 0, :] for _ in range(2)], "dve": [R2.alloc([1, 528], F32)[:, 0, :] for _ in range(2)]}
        plb = [R2.alloc([1, 512], BF16)[:, 0, :] for _ in range(3)]
        qrb = [R2.alloc([8, 64], BF16) for _ in range(3)]
        rt1 = [R2.alloc([8, 16], F32) for _ in range(3)]
        rt2 = [R2.alloc([8, 16], F32) for _ in range(3)]
        wpool = R2.alloc([4, 128], BF16)
        ytmp = [R2.alloc([1, D], F32)[:, 0, :] for _ in range(2)]
        QT = XA.alloc([4, S], BF16)
        KT = XA.alloc([4, S], BF16)
        Vflat = XA.alloc([NT * 4, 130], BF16)
        V = Vflat.rearrange("p (a b) c -> p a b c", a=NT)
        PT = [XA.alloc([2, 512], BF16) for _ in range(3)]
        MA_lo = R2.alloc([1, 128], BF16)[:, 0, :]
        MA_hi = R2.alloc([1, 128], BF16)[:, 0, :]
        ID_lo = R2.alloc([1, 128], BF16)[:, 0, :]
        ID_hi = R2.alloc([1, 128], BF16)[:, 0, :]
        NEG = -30000.0
        mk = [K("maskc")]
        for (t, lo, hi, base) in ((MA_lo, 0, 64, 0), (MA_lo, 64, 128, 0), (MA_hi, 0, 64, -64), (MA_hi, 64, 128, -64)):
            sch.add("pool", (lambda t, lo, hi: lambda e: e.memset(t[lo:hi, :], 0.0))(t, lo, hi), reads=mk, writes=mk)
            sch.add("pool", (lambda t, lo, hi, base: lambda e: e.affine_select(out=t[lo:hi, :], in_=t[lo:hi, :], pattern=[[-1, 128]], compare_op=ALU.is_ge,
                                                                                   fill=NEG, base=-base, channel_multiplier=1))(t, lo, hi, base), reads=mk, writes=mk)
        for (t, lo, hi, base) in ((ID_lo, 0, 64, 0), (ID_lo, 64, 128, 0), (ID_hi, 0, 64, -64), (ID_hi, 64, 128, -64)):
            sch.add("pool", (lambda t, lo, hi: lambda e: e.memset(t[lo:hi, :], 0.0))(t, lo, hi), reads=mk, writes=mk)
            sch.add("pool", (lambda t, lo, hi, base: lambda e: e.affine_select(out=t[lo:hi, :], in_=t[lo:hi, :], pattern=[[-1, 128]], compare_op=ALU.not_equal,
                                                                                   fill=1.0, base=-base, channel_multiplier=1))(t, lo, hi, base), reads=mk, writes=mk)
        finset = [dict(T9=XA.alloc([9, 128], F32), da4=XA.alloc([4, 128], F32), dabb4=XA.alloc([4, 128], BF16), ss=ss_d, vv=vv_d, rstd=rstd_d, rr9=rr9),
                  dict(T9=R2.alloc([9, 128], F32), da4=R2.alloc([4, 128], F32), dabb4=R2.alloc([4, 128], BF16), ss=ss_d2, vv=vv_d2, rstd=rstd_d2, rr9=rr9b)]
        XRK = K("XR")
        for nm in ("QT", "KT", "V", "PT", "T9_0", "da4_0", "dabb4_0"):
            sch.group_of[nm] = "XR"
        for nm in ("T9_1", "da4_1", "dabb4_1"):
            sch.group_of[nm] = "R2"

        sch.add("pool", lambda e: e.memset(Vflat[:, :, 128:129], 1.0), writes=[K("V", "ones")])
        load_gb("g_mix_pre")
        def load_slab1(si, w_ap, c0, c1):
            load_wslab("wsl%d" % si, wsl[si][:, :, :], w_ap, c0, c1, K("wsl", si))

        def start_loads(it):
            if it == 0:
                load_slab1(1, w_in_d, 0, 512)
            elif it == 1:
                load_slab1(0, w_in_d, 1024, 1536)
            elif it == 2:
                load_slab1(2, w_in_d, 512, 1024)
            elif it == 3:
                dma_load("pool", "wpool", wpool[:, :, :], wpool_d[:, :, :], writes=[K("wpool")])
            elif it == 5:
                for ce in ("pool", "dve"):
                    for ti in range(2):
                        sch.add("pool", (lambda t: lambda e: e.memset(t[:, :], 0.0))(Tb[ce][ti]), writes=[K("T", ce, ti)])

        sch.add("act", lambda e: e.activation(out=warm[:, 1:2], in_=warm[:, 0:1], func=AF.Square), reads=[K("warm")], writes=[K("warm")])
        setup_tables()
        conv_q = []
        cv_n = [0]

        def cv_chain():
            i = cv_n[0]
            cv_n[0] += 1
            return ([K("cvchain", i - 8)] if i >= 8 else []), [K("cvchain", i)]

        slab_plan2 = [(w_xkv_d, 0, 512), (w_xkv_d, 512, 1024), (w_xkv_d, 1024, 1536), (w_xkv_d, 1536, 2048), (w_xq_d, 0, 512), (w_xq_d, 512, 1024),
                      (w_xo_d, 0, 512), (w_xo_d, 512, 1024)]
        for s_ in range(11):
            for (nm, w_d, scr) in (("wg", w_gate_d, wg_s), ("wu", w_up_d, wu_s)):
                def cv(s_=s_, nm=nm, w_d=w_d, scr=scr):
                    sn = slot("cv_%s%d" % (nm, s_))
                    rd, wr = cv_chain()
                    sch.add("pool", lambda e: e.dma_start(out=scr[s_, :, :].rearrange("p (a b) -> p a b", a=KC), in_=w_cols(w_d, s_ * 256, (s_ + 1) * 256)),
                            reads=rd, writes=[K(nm + "scr", s_)] + wr, dma=sn)
                conv_q.append(cv)
        for hf in range(4):
            def cvd(hf=hf):
                sn = slot("cv_wd%d" % hf)
                f0, f1 = (0, 6, 12, 17, 22)[hf], (0, 6, 12, 17, 22)[hf + 1]
                src = w_down_d.rearrange("(fc p) n -> p fc n", p=128)[:, f0:f1, :]
                dst = wd_s.rearrange("p (a b) -> p a b", a=NFC)[:, f0:f1, :]
                rd, wr = cv_chain()
                sch.add("pool", lambda e: e.dma_start(out=dst, in_=src), reads=rd, writes=[K("wdscr", hf)] + wr, dma=sn)
            conv_q.append(cvd)

        def drain_conv(n):
            for _ in range(n):
                if conv_q:
                    conv_q.pop(0)()

        qk_pipe = Pipe(2)
        qcount = [0]

        def proj_tile(ci, tt, sl):
            bank = next_bank()
            mm(ps[:, bank, :], [(actT[:, kc, tt * 128:(tt + 1) * 128], wsl[sl][:, kc, :]) for kc in range(KC)],
               reads=[K("wsl", sl)] + actT_keys(range(KC), [tt]), writes=[K("ps", bank)])
            if ci == 1:
                drain_conv(1)
            if ci == 2:
                evac("act", V[:, tt, :, 0:128], ps[:, bank, :].rearrange("p (a b) -> p a b", a=4), reads=[K("ps", bank)], writes=[K("V", tt)])
                return
            psv = ps[:, bank, :].rearrange("p (a b) -> p a b", a=8)
            qs = qcount[0] % 3
            qcount[0] += 1
            qr = qrb[qs]
            qk = K("qr", qs)
            import os
            if "norope" in os.environ.get("KSKIP", ""):
                sch.add("act", (lambda qr, psv: lambda e: e.activation(out=qr[:, :, :], in_=psv[:, :, :], func=AF.Copy))(qr, psv),
                        reads=[K("ps", bank)], writes=[qk])
            else:
                sch.add("act", (lambda qr, psv: lambda e: e.activation(out=qr[:, :, 16:64], in_=psv[:, :, 16:64], func=AF.Copy))(qr, psv),
                        reads=[K("ps", bank)], writes=[qk])
                ccb = cc[:, tt:tt + 1, :].to_broadcast([128, 8, 16])
                nsa = ns[:, tt:tt + 1, 0:8].to_broadcast([128, 8, 8])
                nsb = ns[:, tt:tt + 1, 8:16].to_broadcast([128, 8, 8])
                a1, a2 = rt1[qs], rt2[qs]
                sch.add("dve", (lambda a1, psv, ccb: lambda e: e.tensor_tensor(out=a1[:, :, :], in0=psv[:, :, 0:16], in1=ccb, op=ALU.mult))(a1, psv, ccb),
                        reads=[K("ps", bank), K("cc")], writes=[K("rt1", qs)])
                sch.add("dve", (lambda a2, psv, nsa: lambda e: e.tensor_tensor(out=a2[:, :, 0:8], in0=psv[:, :, 8:16], in1=nsa, op=ALU.mult))(a2, psv, nsa),
                        reads=[K("ps", bank), K("ns")], writes=[K("rt2", qs)])
                sch.add("dve", (lambda a2, psv, nsb: lambda e: e.tensor_tensor(out=a2[:, :, 8:16], in0=psv[:, :, 0:8], in1=nsb, op=ALU.mult))(a2, psv, nsb),
                        reads=[K("ps", bank), K("ns")], writes=[K("rt2", qs)])
                sch.add("dve", (lambda qr, a1, a2: lambda e: e.tensor_tensor(out=qr[:, :, 0:16], in0=a1[:, :, :], in1=a2[:, :, :], op=ALU.add))(qr, a1, a2),
                        reads=[K("rt1", qs), K("rt2", qs)], writes=[qk])
            def back_qk(qr=qr, qk=qk, ci=ci, tt=tt):
                bank2 = next_bank()
                pb = psb(bank2)
                qrf = qr.rearrange("p a b -> p (a b)")

                def tfn(e):
                    ins = None
                    for h in range(4):
                        ins = e.transpose(out=pb[:, h, :], in_=qrf[:, h * 128:(h + 1) * 128], identity=ident[:, :])
                    return ins
                sch.add("pe", tfn, reads=[qk, K("ident")], writes=[K("ps", bank2)])
                dstT = QT if ci == 0 else KT
                evac("act", dstT[:, :, tt * 128:(tt + 1) * 128], pb[:, 0:4, :], reads=[K("ps", bank2)], writes=[K("QT" if ci == 0 else "KT", tt)])
            qk_pipe.push(back_qk)

        pipe = Pipe(1)
        for tt in range(NT):
            xs = xin[tt % 4]
            dma_load("sp", "xin%d" % (tt % 4), xs, x_d[tt * 128:(tt + 1) * 128, :], writes=[K("xin", tt % 4)])
            pipe.push(norm_T("n1", tt, xs, [K("xin", tt % 4)], actT[:, :, tt * 128:(tt + 1) * 128], actT_keys(range(KC), [tt]), junk=ytmp[tt % 2], junkk=K("ytmp", tt % 2), ev="dve"))
            start_loads(tt)
            drain_setup(10)
            if tt >= 4:
                proj_tile(0, tt - 4, 1)
                proj_tile(2, tt - 4, 0)
        pipe.flush()
        drain_setup(1000)
        if upto == "1a":
            return _finish(nc, es, sch, dma_slots, dbg_d, [(actT[:, :, :].rearrange("p a b -> p (a b)"), actT_keys(range(KC), range(NT)), KC * S, BF16)], locals())
        for tt in range(NT - 4, NT):
            proj_tile(0, tt, 1)
        load_slab1(1, w_in_d, 1536, 2048)
        for tt in range(NT - 4, NT):
            proj_tile(2, tt, 0)
        load_slab1(0, w_out_d, 0, 512)
        for tt in range(3):
            proj_tile(1, tt, 2)

        s_u = 1
        ui = 0
        pipe = Pipe(2)
        unit = 0
        unit_order = [(g_, tg_) for (ga, gb_) in ((0, 1), (3, 2)) for tg_ in range(4) for g_ in (ga, gb_)]
        for (g, tg) in unit_order:
            w = 2 << g
            ceng = "pool" if g in (0, 3) else "dve"
            if True:
                bank = next_bank()
                mm(ps[:, bank, :], [(wsl[s_u][:, kc, g * 128:(g + 1) * 128], actT[:, kc, tg * 512:(tg + 1) * 512]) for kc in range(KC)],
                   reads=[K("wsl", s_u)] + actT_keys(range(KC), range(tg * 4, tg * 4 + 4)), writes=[K("ps", bank)])
                us = ui % 2
                ui += 1
                U = Ub[us]
                hal = halo[:, g, :]
                if tg == 0:
                    sch.add("pool", (lambda U: lambda e: e.memset(U[:, 0:16], 0.0))(U), writes=[K("U", us)])
                else:
                    sch.add("pool", (lambda U, hal: lambda e: e.tensor_copy(out=U[:, 0:16], in_=hal))(U, hal), reads=[K("halo", g)], writes=[K("U", us)])
                evac("act", U[:, 16:528], ps[:, bank, :], reads=[K("ps", bank)], writes=[K("U", us)])
                if tg < 3:
                    sch.add("pool", (lambda U, hal: lambda e: e.tensor_copy(out=hal, in_=U[:, 512:528]))(U, hal), reads=[K("U", us)], writes=[K("halo", g)])
                src = U
                srck = K("U", us)
                for lvl in range(g + 1):
                    sh = 1 << lvl
                    dst = Tb[ceng][lvl % 2]
                    dstk = K("T", ceng, lvl % 2)
                    sch.add(ceng, (lambda dst, src, sh: lambda e: e.tensor_tensor(out=dst[:, sh:528], in0=src[:, sh:528], in1=src[:, 0:528 - sh], op=ALU.add))(dst, src, sh),
                            reads=[srck], writes=[dstk])
                    src, srck = dst, dstk
                pls = ui % 3
                pl = plb[pls]
                sch.add("dve", (lambda pl, src, U, w: lambda e: e.scalar_tensor_tensor(out=pl[:, :], in0=src[:, 16:528], scalar=1.0 / w, in1=U[:, 16:528], op0=ALU.mult, op1=ALU.subtract))(pl, src, U, w),
                        reads=[srck, K("U", us)], writes=[K("pl", pls)])
                if tg == 0:
                    sch.add("dve", (lambda src, g: lambda e: e.tensor_tensor(out=t16, in0=src[:, 16:32], in1=rc[:, g, :], op=ALU.mult))(src, g),
                            reads=[srck, K("rc")], writes=[K("t16")])
                    sch.add("dve", (lambda pl, U: lambda e: e.tensor_tensor(out=pl[:, 0:16], in0=t16, in1=U[:, 16:32], op=ALU.subtract))(pl, U),
                            reads=[K("t16"), K("U", us)], writes=[K("pl", pls)])
                def back_pool(g=g, tg=tg, pl=pl, pls=pls):
                    bank2 = next_bank()
                    mm(ps[:, bank2, :], [(wpool[:, g, :], pl[:, :])], reads=[K("wpool"), K("pl", pls)], writes=[K("ps", bank2)])
                    evac("act", poT[:, g, tg * 512:(tg + 1) * 512], ps[:, bank2, :], reads=[K("ps", bank2), K("pscale")], writes=[K("poT", g, tg)], scale=pscale[:, g:g + 1])
                pipe.push(back_pool)
                if unit + 3 < NT:
                    proj_tile(1, unit + 3, 2)
                unit += 1
        pipe.flush()
        qk_pipe.flush()
        load_slab1(1, w_out_d, 512, 1024)
        if upto == "1b":
            return _finish(nc, es, sch, dma_slots, dbg_d, [(poT[:, :, :].rearrange("p a b -> p (a b)"), [K("poT", g, tg) for g in range(4) for tg in range(4)], 4 * S, BF16)], locals())

        if upto == "1c":
            return _finish(nc, es, sch, dma_slots, dbg_d,
                           [(QT[:, :, :].rearrange("p a b -> p (a b)"), [K("QT", tt) for tt in range(NT)], 4 * S, BF16),
                            (KT[:, :, :].rearrange("p a b -> p (a b)"), [K("KT", tt) for tt in range(NT)], 4 * S, BF16),
                            (V[:, :, :, :].rearrange("p a b c -> p (a b c)"), [K("V", tt) for tt in range(NT)] + [K("V", "ones")], NT * 4 * 130, BF16)], locals())

        steps = [(h, G, j) for h in range(4) for G in range(4) for j in range(4 * G + 4)]

        def st_info(step):
            h, G, j = step
            q0 = max(4 * G, j)
            n = (4 * G + 4 - q0) * 128
            return q0, n

        def emit_ST(i):
            h, G, j = steps[i]
            q0, n = st_info(steps[i])
            p = i % 2
            qks = [K("QT", t) for t in range(q0, 4 * G + 4)] + [K("KT", j)]

            diag = j >= 4 * G

            def fn(e):
                ins = None
                for m in range(2):
                    r0, r1 = 64 * m, 64 * m + 64
                    ins = e.matmul(ps[0:128, 2 * p + m, 0:n], KT[r0:r1, h, j * 128:(j + 1) * 128], QT[r0:r1, h, q0 * 128:q0 * 128 + n], start=True, stop=True)
                    if diag:
                        e.matmul(ps[0:128, 2 * p + m, 0:128], MA_lo[r0:r1, :], ID_lo[r0:r1, :], start=False, stop=False, skip_group_check=True)
                        ins = e.matmul(ps[0:128, 2 * p + m, 0:128], MA_hi[r0:r1, :], ID_hi[r0:r1, :], start=False, stop=True, skip_group_check=True)
                return ins
            sch.add("pe", fn, reads=qks + ([K("maskc")] if diag else []), writes=[K("ps", 2 * p), K("ps", 2 * p + 1)])

        pend_q = []
        fin_cnt = [0]

        late_T = []

        def run_pend(force=False, hold_T=False):
            for it in list(pend_q):
                it[0] -= 1
                if it[0] <= 0 or force:
                    pend_q.remove(it)
                    if hold_T and len(it) > 2:
                        late_T.append(it[1])
                    else:
                        it[1]()
        sch.add("dve", lambda e: e.memset(ps[:, 6, 258:387], 1.0), writes=[K("ps", 6)])
        emit_ST(0)
        emit_ST(1)
        for i, (h, G, j) in enumerate(steps):
            q0, n = st_info(steps[i])
            p = i % 2
            pt = PT[i % 3]
            ptk = K("PT", i % 3)
            sch.add("act", (lambda pt, p, n: lambda e: e.activation(out=pt[:, :, 0:n], in_=ps[:, 2 * p:2 * p + 2, 0:n], func=AF.Exp, scale=0.125))(pt, p, n),
                    reads=[K("ps", 2 * p), K("ps", 2 * p + 1)], writes=[ptk])
            if i + 2 < len(steps):
                emit_ST(i + 2)

            def pvfn(e, pt=pt, q0=q0, n=n, G=G, j=j, h=h):
                ins = None
                for qi in range(n // 128):
                    il = q0 + qi - 4 * G
                    for m in range(2):
                        a = il * 2 + m
                        bank = 4 + a // 3
                        off = (a % 3) * 129
                        ins = e.matmul(ps[:, bank, off:off + 129], pt[:, m, qi * 128:(qi + 1) * 128], V[:, j, h, 0:129],
                                       start=(j == 0 and a % 3 == 0), stop=(j == 4 * G + il), skip_group_check=True)
                return ins
            sch.add("pe", pvfn, reads=[ptk, K("V", j), K("V", "ones")], writes=[K("ps", 4), K("ps", 5), K("ps", 6)])
            run_pend()
            if i % 4 == 3:
                drain_conv(1)
            if j == 4 * G + 3:
                fi = fin_cnt[0] % 2
                fin_cnt[0] += 1
                fs = finset[fi]
                T9, da4, dabb4, rr9c, ssd, vvd, rsd = fs["T9"], fs["da4"], fs["dabb4"], fs["rr9"], fs["ss"], fs["vv"], fs["rstd"]
                kT9, kda, kdb, krr = K("T9_%d" % fi), K("da4_%d" % fi), K("dabb4_%d" % fi), K("rr9", fi)
                acck = [K("ps", 4), K("ps", 5), K("ps", 6)]
                acc4 = ps[:, 4:7, 0:387].rearrange("p b (s c) -> p b s c", c=129)
                rr9v = rr9c.rearrange("p (b s) -> p b s", b=3)
                sch.add("dve", (lambda acc4, rr9v: lambda e: e.reciprocal(out=rr9v, in_=acc4[:, :, :, 128]))(acc4, rr9v), reads=acck, writes=[krr])
                rr8 = rr9c[:, 0:8].rearrange("p (a b) -> p a b", b=2)
                sch.add("dve", (lambda rr8: lambda e: e.tensor_scalar(out=rr8[:, :, 1], in0=rr8[:, :, 1], scalar1=neg_lam, scalar2=None, op0=ALU.mult))(rr8),
                        reads=[krr, K("neg_lam")], writes=[krr])
                T9v = T9.rearrange("p (b s) c -> p b s c", b=3)
                wbc = rr9v.unsqueeze(3).to_broadcast([128, 3, 3, 128])
                sch.add("dve", (lambda T9v, acc4, wbc: lambda e: e.tensor_tensor(out=T9v, in0=acc4[:, :, :, 0:128], in1=wbc, op=ALU.mult))(T9v, acc4, wbc),
                        reads=acck + [krr], writes=[kT9])
                T8 = T9[:, 0:8, :].rearrange("p (a b) c -> p a b c", b=2)
                sch.add("dve", (lambda T8, da4: lambda e: e.tensor_tensor(out=da4[:, :, :], in0=T8[:, :, 0, :], in1=T8[:, :, 1, :], op=ALU.add))(T8, da4),
                        reads=[kT9], writes=[kda])
                sq = T9[:, 0:4, :]
                tmpn = T9[:, 4:8, :]

                def fin_part2(sq=sq, tmpn=tmpn, da4=da4, dabb4=dabb4, ssd=ssd, vvd=vvd, rsd=rsd, kda=kda, kT9=kT9, kdb=kdb, fi=fi, h=h, G=G):
                  sch.add("dve", (lambda sq, da4: lambda e: e.tensor_tensor(out=sq, in0=da4[:, :, :], in1=da4[:, :, :], op=ALU.mult))(sq, da4), reads=[kda], writes=[kT9])
                  sch.add("dve", (lambda sq, ssd: lambda e: e.tensor_reduce(out=ssd, in_=sq, axis=mybir.AxisListType.X, op=ALU.add))(sq, ssd), reads=[kT9], writes=[K("ss", ("sub", fi))])
                  rstd_chain(ssd, vvd, rsd, 128, ("sub", fi))

                  def fin_part2b():
                    rbc = rsd.unsqueeze(2).to_broadcast([128, 4, 128])
                    gbc = gsub8[:, :].unsqueeze(1).to_broadcast([128, 4, 128])
                    sch.add("dve", (lambda tmpn, rbc, da4: lambda e: e.tensor_tensor(out=tmpn, in0=da4[:, :, :], in1=rbc, op=ALU.mult))(tmpn, rbc, da4),
                            reads=[kda, K("rstd", ("sub", fi))], writes=[kT9])
                    sch.add("dve", (lambda tmpn, gbc, dabb4: lambda e: e.tensor_tensor(out=dabb4[:, :, :], in0=tmpn, in1=gbc, op=ALU.mult))(tmpn, gbc, dabb4),
                            reads=[kT9, K("gsub")], writes=[kdb])
                    pend_q.append([3, fin_T, "T"])

                  def fin_T(h=h, G=G, dabb4=dabb4, kdb=kdb):
                      pb = psb(7)

                      def tfn2(e):
                          ins = None
                          for il in range(4):
                              ins = e.transpose(out=pb[:, il, :], in_=dabb4[:, il, :], identity=ident[:, :])
                          return ins
                      sch.add("pe", tfn2, reads=[kdb, K("ident")], writes=[K("ps", 7)])
                      evac("dve", actT[:, h, G * 512:(G + 1) * 512].rearrange("p (a b) -> p a b", a=4), pb[:, 0:4, :], reads=[K("ps", 7)],
                           writes=actT_keys([h], range(4 * G, 4 * G + 4)))
                  pend_q.append([1, fin_part2b])
                nxt_short = (i + 1 < len(steps)) and steps[i + 1][1] == 0
                pend_q.append([5 if nxt_short else 1, fin_part2])
        while pend_q:
            run_pend(force=True, hold_T=(upto != "1d"))
        if upto == "1d":
            return _finish(nc, es, sch, dma_slots, dbg_d, [(actT[:, 0:4, :].rearrange("p a b -> p (a b)"), actT_keys(range(4), range(NT)), 4 * S, BF16)], locals())

        load_gb("g_mix_post")
        pipe = Pipe(1)
        for tt in range(NT):
            xs = xin[tt % 4]
            dma_load("sp", "xin%d" % (tt % 4), xs, x_d[tt * 128:(tt + 1) * 128, :], writes=[K("xin", tt % 4)])
            if tt == 3:
                while late_T:
                    late_T.pop(0)()
            b0 = next_pair()
            for cg in range(2):
                pairs = [(actT[:, kc, tt * 128:(tt + 1) * 128], wsl[cg][:, kc, :]) for kc in range(4)] + \
                        [(poT[:, kc, tt * 128:(tt + 1) * 128], wsl[cg][:, 4 + kc, :]) for kc in range(4)]
                mm(ps[:, b0 + cg, :], pairs, reads=[K("wsl", cg)] + actT_keys(range(4), [tt]) + [K("poT", g, tt // 4) for g in range(4)], writes=[K("ps", b0 + cg)])
            pipe.push(post_norm_res("p1", tt, b0, xs, [K("xin", tt % 4)], X[:, tt, :], [K("X", tt), XRK], ytmp[tt % 2], K("ytmp", tt % 2)))
        pipe.flush()
        if upto == "1e":
            return _finish(nc, es, sch, dma_slots, dbg_d, [(X[:, :, :].rearrange("p a b -> p (a b)"), [K("X", tt) for tt in range(NT)], NT * D, F32)], locals())

        R2.reset()
        xqT = R2.alloc([KC, S], BF16)
        wsl2 = [R2.alloc([KC, 512], BF16) for _ in range(2)]
        KcT = R2.alloc([KC, MEM], BF16)
        Vcflat = R2.alloc([2 * 4, 258], BF16)
        Vc = Vcflat.rearrange("p (a b) c -> p a b c", a=2)
        memT = R2.alloc([KC, MEM], BF16)
        mem_in = [R2.alloc([1, D], F32)[:, 0, :] for _ in range(2)]
        PT2 = [R2.alloc([2, 512], BF16) for _ in range(2)]
        xo_tms = [R2.alloc([4, D], BF16) for _ in range(2)]
        ytmp = [R2.alloc([1, D], F32)[:, 0, :] for _ in range(2)]
        phase_barrier()

        def P2(keys):
            return list(keys)

        slab_j = [0]

        def next_slab2():
            i = slab_j[0]
            slab_j[0] += 1
            w_ap, c0, c1 = slab_plan2[i]
            load_wslab("wslb%d" % (i % 2), wsl2[i % 2][:, :, :], w_ap, c0, c1, K("wsl2", i % 2))
            return i % 2

        next_slab2()
        next_slab2()
        sch.add("pool", lambda e: e.memset(Vcflat[:, :, 256:257], 1.0), writes=[K("Vc", "ones")])
        load_gb("g_x_pre")
        gmem_t = xo_tms[0][:, 0:2, :].rearrange("p a b -> p (a b)").bitcast(F32)
        gmem_keys = [K("xo_tm", 0, qi, hh) for qi in range(2) for hh in range(4)]
        dma_load("sp", "gmem", gmem_t, g_d["g_mem"][0:1, :].to_broadcast([128, D]), writes=gmem_keys)
        for mt in range(2):
            dma_load("sp", "memin%d" % mt, mem_in[mt], mem_d[mt * 128:(mt + 1) * 128, :], writes=[K("mem_in", mt)])
        for tt in range(NT):
            norm_stats(tt, X[:, tt, :], [K("X", tt)], ytmp[tt % 2], K("ytmp2", tt % 2))
        xpipe = Pipe(1)
        xt = [0]

        def x_tile():
            if xt[0] < NT:
                tt = xt[0]
                xt[0] += 1
                xpipe.push(norm_scale(tt, X[:, tt, :], [K("X", tt)], actT[:, :, tt * 128:(tt + 1) * 128], actT_keys(range(KC), [tt]), ev="act"))

        for _ in range(4):
            x_tile()
        xpipe.flush()
        for mt in range(2):
            norm_T("nm", mt, mem_in[mt], [K("mem_in", mt)], memT[:, :, mt * 128:(mt + 1) * 128], [K("memT", mt)], junk=ytmp[mt % 2], junkk=K("ytmp2", mt % 2),
                   gain=gmem_t, gkeys=gmem_keys)()
        memk = [K("memT", 0), K("memT", 1)]
        for sl in range(2):
            for ocl in range(4):
                oc = sl * 4 + ocl
                bank = next_bank()
                mm(ps[:, bank, 0:MEM], [(wsl2[sl][:, kc, ocl * 128:(ocl + 1) * 128], memT[:, kc, :]) for kc in range(KC)], reads=[K("wsl2", sl)] + memk, writes=[K("ps", bank)])
                evac(alt_eng(), KcT[:, oc, :], ps[:, bank, 0:MEM], reads=[K("ps", bank)], writes=[K("KcT", oc)])
                x_tile()
            next_slab2()
        for sl in range(2):
            for mt in range(2):
                bank = next_bank()
                mm(ps[:, bank, :], [(memT[:, kc, mt * 128:(mt + 1) * 128], wsl2[sl][:, kc, :]) for kc in range(KC)], reads=[K("wsl2", sl)] + memk, writes=[K("ps", bank)])
                evac(alt_eng(), Vc[:, mt, 2 * sl:2 * sl + 2, 0:256], ps[:, bank, :].rearrange("p (a b) -> p a b", a=2), reads=[K("ps", bank)], writes=[K("Vc", mt, sl)])
                x_tile()
            next_slab2()
        while xt[0] < NT:
            x_tile()
        xpipe.flush()
        for sl in range(2):
            for ocl in range(4):
                oc = sl * 4 + ocl
                for tg in range(4):
                    bank = next_bank()
                    mm(ps[:, bank, :], [(wsl2[sl][:, kc, ocl * 128:(ocl + 1) * 128], actT[:, kc, tg * 512:(tg + 1) * 512]) for kc in range(KC)],
                       reads=[K("wsl2", sl)] + actT_keys(range(KC), range(4 * tg, 4 * tg + 4)), writes=[K("ps", bank)])
                    evac(alt_eng(), xqT[:, oc, tg * 512:(tg + 1) * 512], ps[:, bank, :], reads=[K("ps", bank)], writes=[K("xqT", oc, tg)])
            next_slab2()
        if upto == "2c":
            return _finish(nc, es, sch, dma_slots, dbg_d,
                           [(xqT[:, :, :].rearrange("p a b -> p (a b)"), [K("xqT", oc, tg) for oc in range(8) for tg in range(4)], KC * S, BF16),
                            (KcT[:, :, :].rearrange("p a b -> p (a b)"), [K("KcT", oc) for oc in range(8)], KC * MEM, BF16),
                            (Vc[:, :, :, :].rearrange("p a b c -> p (a b c)"), [K("Vc", mt, sl) for mt in range(2) for sl in range(2)] + [K("Vc", "ones")], 2 * 4 * 258, BF16)], locals())
        xsteps = [(tg, h) for tg in range(4) for h in range(4)]
        sbank_rr = [0]

        def emit_SC(i, mt):
            tg, h = xsteps[i]
            sbk = (2 * i + mt) % 3
            mm(ps[:, sbk, :], [(KcT[:, 2 * h + dc, mt * 128:(mt + 1) * 128], xqT[:, 2 * h + dc, tg * 512:(tg + 1) * 512]) for dc in range(2)],
               reads=[K("KcT", 2 * h), K("KcT", 2 * h + 1), K("xqT", 2 * h, tg), K("xqT", 2 * h + 1, tg)], writes=[K("ps", sbk)])
            pt2 = PT2[i % 2]
            sch.add("act", (lambda pt2, sbk, mt: lambda e: e.activation(out=pt2[:, mt, :], in_=ps[:, sbk, :], func=AF.Exp, scale=1.0 / 16.0))(pt2, sbk, mt),
                    reads=[K("ps", sbk)], writes=[K("PT2", i % 2, mt)])

        pend_X = [None]
        emit_SC(0, 0)
        emit_SC(0, 1)
        for i, (tg, h) in enumerate(xsteps):
            if i + 1 < len(xsteps):
                emit_SC(i + 1, 0)
                emit_SC(i + 1, 1)
            pt2 = PT2[i % 2]
            xo_tm = xo_tms[tg % 2]
            xb = tg % 2

            def pv_half(hf, pt2=pt2, h=h, i=i):
                for qi in (2 * hf, 2 * hf + 1):
                    mm(ps[:, 4 + qi, 0:257], [(pt2[:, mt, qi * 128:(qi + 1) * 128], Vc[:, mt, h, 0:257]) for mt in range(2)],
                       reads=[K("PT2", i % 2, 0), K("PT2", i % 2, 1), K("Vc", 0, h // 2), K("Vc", 1, h // 2), K("Vc", "ones")], writes=[K("ps", 4 + qi)])

            def fin_half(hf, h=h, xo_tm=xo_tm, xb=xb):
                accs = [K("ps", 4 + 2 * hf), K("ps", 5 + 2 * hf)]
                rrh = rr2[:, 2 * hf:2 * hf + 2]
                sch.add("dve", (lambda rrh, hf: lambda e: e.reciprocal(out=rrh.unsqueeze(2), in_=ps[:, 4 + 2 * hf:6 + 2 * hf, 256:257]))(rrh, hf), reads=accs, writes=[K("rr2", hf)])
                r2bc = rrh.unsqueeze(2).to_broadcast([128, 2, 256])
                sch.add("dve", (lambda h, hf, r2bc: lambda e: e.tensor_tensor(out=xo_tm[:, 2 * hf:2 * hf + 2, h * 256:(h + 1) * 256], in0=ps[:, 4 + 2 * hf:6 + 2 * hf, 0:256], in1=r2bc, op=ALU.mult))(h, hf, r2bc),
                        reads=accs + [K("rr2", hf)], writes=[K("xo_tm", xb, qi, h) for qi in (2 * hf, 2 * hf + 1)])

            pv_half(0)
            fin_half(0)
            pv_half(1)
            if pend_X[0] is not None:
                pend_X[0]()
                pend_X[0] = None
            fin_half(1)
            if h == 3:
                def fin_X(tg=tg, xo_tm=xo_tm, xb=xb):
                    for qi in range(4):
                        tt = tg * 4 + qi
                        pb = psb(3)

                        def tfn3(e, pb=pb, qi=qi):
                            ins = None
                            for kc in range(KC):
                                ins = e.transpose(out=pb[:, kc, :], in_=xo_tm[:, qi, kc * 128:(kc + 1) * 128], identity=ident[:, :])
                            return ins
                        sch.add("pe", tfn3, reads=[K("xo_tm", xb, qi, hh) for hh in range(4)] + [K("ident")], writes=[K("ps", 3)])
                        evac(alt_eng(), actT[:, :, tt * 128:(tt + 1) * 128], pb[:, :, :], reads=[K("ps", 3)], writes=actT_keys(range(KC), [tt]))
                pend_X[0] = fin_X
        if upto == "2d" and pend_X[0] is not None:
            pend_X[0]()
            pend_X[0] = None
        if upto == "2d":
            return _finish(nc, es, sch, dma_slots, dbg_d, [(actT[:, :, :].rearrange("p a b -> p (a b)"), actT_keys(range(KC), range(NT)), KC * S, BF16)], locals())
        load_gb("g_x_post")
        pipe = Pipe(1)
        for tt in range(NT):
            if tt == 2 and pend_X[0] is not None:
                pend_X[0]()
                pend_X[0] = None
            b0 = next_pair()
            for cg in range(2):
                mm(ps[:, b0 + cg, :], [(actT[:, kc, tt * 128:(tt + 1) * 128], wsl2[cg][:, kc, :]) for kc in range(KC)],
                   reads=[K("wsl2", cg)] + actT_keys(range(KC), [tt]), writes=[K("ps", b0 + cg)])
            pipe.push(post_norm_res("p2", tt, b0, X[:, tt, :], [K("X", tt)], X[:, tt, :], [K("X", tt)], ytmp[tt % 2], K("ytmp2", tt % 2)))
        pipe.flush()
        if upto == "2e":
            return _finish(nc, es, sch, dma_slots, dbg_d, [(X[:, :, :].rearrange("p a b -> p (a b)"), [K("X", tt) for tt in range(NT)], NT * D, F32)], locals())

        R2.reset()
        wd = R2.alloc([NFC, D], BF16)
        hidT = R2.alloc([NFC, 512], BF16)
        gsl = [R2.alloc([KC, 256], BF16) for _ in range(2)]
        usl = [R2.alloc([KC, 256], BF16) for _ in range(2)]
        sgb = [R2.alloc([1, 512], F32)[:, 0, :] for _ in range(2)]
        ytmp3 = [R2.alloc([1, D], F32)[:, 0, :] for _ in range(2)]
        phase_barrier()
        load_gb("g_ffn_pre")
        gu_n = [0]

        def load_gu(q4, s):
            i = gu_n[0]
            gu_n[0] += 1
            sl = i % 2
            for (nm, bufs, scr, sk) in (("gsl", gsl, wg_s, "wgscr"), ("usl", usl, wu_s, "wuscr")):
                flat = bufs[sl][:, :, :].rearrange("p a b -> p (a b)")
                dma_load("sp", "%s%d" % (nm, sl), flat, scr[s, :, :], writes=[K(nm, sl)], reads=[K(sk, s)])

        gu_list = [(q4, s) for q4 in range(4) for s in range(11)]
        load_gu(*gu_list[0])
        load_gu(*gu_list[1])
        drain_conv(1000)
        dma_load("sp", "wd", wd[:, :, :].rearrange("p a b -> p (a b)"), wd_s[:, :], writes=[K("wd")], reads=[K("wdscr", hf) for hf in range(4)])
        for tt in range(NT):
            norm_stats(tt, X[:, tt, :], [K("X", tt)], ytmp3[tt % 2], K("ytmp3", tt % 2))
        xpipe3 = Pipe(1)
        x3 = [0]

        def x3_tile():
            if x3[0] < NT:
                tt = x3[0]
                x3[0] += 1
                bank_rr[0] = 4 + (tt % 4)
                xpipe3.push(norm_scale(tt, X[:, tt, :], [K("X", tt)], actT[:, :, tt * 128:(tt + 1) * 128], actT_keys(range(KC), [tt]), ev="act"))
                if x3[0] == NT:
                    bank_rr[0] = 4
                    xpipe3.flush()
                    load_gb("g_ffn_post")

        for _ in range(4):
            x3_tile()
        bank_rr[0] = 4
        xpipe3.flush()
        gi = 0
        stepc = 0
        pipe3 = Pipe(1)
        for q4 in range(4):
            tks = actT_keys(range(KC), range(4 * q4, 4 * q4 + 4))
            for s in range(11):
                sl = gi % 2
                if q4 == 0:
                    x3_tile()
                    if s == 0:
                        x3_tile()
                for f2 in range(2):
                    fc = 2 * s + f2
                    pr = (stepc % 2) * 2
                    stepc += 1
                    mm(ps[:, pr, :], [(gsl[sl][:, kc, f2 * 128:(f2 + 1) * 128], actT[:, kc, q4 * 512:(q4 + 1) * 512]) for kc in range(KC)],
                       reads=[K("gsl", sl)] + tks, writes=[K("ps", pr)])
                    mm(ps[:, pr + 1, :], [(usl[sl][:, kc, f2 * 128:(f2 + 1) * 128], actT[:, kc, q4 * 512:(q4 + 1) * 512]) for kc in range(KC)],
                       reads=[K("usl", sl)] + tks, writes=[K("ps", pr + 1)])
                    sg = sgb[stepc % 2]
                    sgk = K("sg", stepc % 2)
                    sch.add("act", (lambda sg, pr: lambda e: e.activation(out=sg, in_=ps[:, pr, :], func=AF.Silu))(sg, pr), reads=[K("ps", pr)], writes=[sgk])
                    sch.add("dve", (lambda sg, pr, fc: lambda e: e.tensor_tensor(out=hidT[:, fc, :], in0=sg, in1=ps[:, pr + 1, :], op=ALU.mult))(sg, pr, fc),
                            reads=[sgk, K("ps", pr + 1)], writes=[K("hidT", fc)])
                gi += 1
                if gi + 1 < len(gu_list):
                    load_gu(*gu_list[gi + 1])
            for tl in range(4):
                tt = q4 * 4 + tl
                b0 = 4 + (tl % 2) * 2
                for cg in range(2):
                    mm(ps[:, b0 + cg, :], [(hidT[:, fc, tl * 128:(tl + 1) * 128], wd[:, fc, cg * 512:(cg + 1) * 512]) for fc in range(NFC)],
                       reads=[K("wd")] + [K("hidT", fc) for fc in range(NFC)], writes=[K("ps", b0 + cg)])
                def store(tt=tt):
                    slot("ost%d" % (tt % 2))
                    sch.add("sp", lambda e: e.dma_start(out=out_d[tt * 128:(tt + 1) * 128, :], in_=X[:, tt, :]), reads=[K("X", tt)], writes=[K("outd", tt)], dma="ost%d" % (tt % 2))
                pipe3.push(post_norm_res("p3", tt, b0, X[:, tt, :], [K("X", tt)], X[:, tt, :], [K("X", tt)], ytmp3[tt % 2], K("ytmp3", tt % 2), after=store))
        pipe3.flush()
        sch.final_slots = ["ost0", "ost1"]
        return _finish(nc, es, sch, dma_slots, None, [], locals())


def _finish(nc, es, sch, dma_slots, dbg_d, dumps, env):
    if dbg_d is not None and dumps:
        r2_t = env["r2_t"]
        off = 0
        for i, (ap, keys, n, dt) in enumerate(dumps):
            words = n if dt == F32 else n // 2
            src = ap if dt == F32 else ap.bitcast(F32)
            name = "dbg%d" % i
            if name not in dma_slots:
                dma_slots.append(name)
            sch.add("sp", (lambda src, off, words: lambda e: e.dma_start(out=dbg_d[:, off:off + words], in_=src))(src, off, words), reads=keys, writes=[("dbgout", i)], dma=name)
            sch.final_slots.append(name)
            off += words
    eng_sem = {}
    dma_sem = {}
    for e in Sched.ENGS:
        eng_sem[e] = es.enter_context(nc.semaphore("s_" + e))
    for s in dma_slots:
        dma_sem[s] = es.enter_context(nc.semaphore("d_" + s))
    block = es.enter_context(nc.Block())
    sch.emit(nc, block, eng_sem, dma_sem)
    return nc


def make_in_maps(inputs):
    f = lambda a: np.ascontiguousarray(np.asarray(a), dtype=np.float32)
    x = f(inputs["x"])
    mem = f(inputs["mem"])
    pos = np.asarray(inputs["positions"]).astype(np.int32)
    lamv = np.zeros((1, 256), np.float32)
    lamv[0, 0:64] = f(inputs["lambda_q1"])
    lamv[0, 64:128] = f(inputs["lambda_k1"])
    lamv[0, 128:192] = f(inputs["lambda_q2"])
    lamv[0, 192:256] = f(inputs["lambda_k2"])
    shared = {
        "lamv": lamv,
        "g_subln": f(inputs["g_subln"]).reshape(1, 128),
        "w_pool": np.ascontiguousarray(f(inputs["w_pool"]).transpose(1, 0, 2)),
        "pool_scale": np.ascontiguousarray(f(inputs["pool_scale"]).reshape(4, 128).T),
    }
    for n in ("g_mix_pre", "g_mix_post", "g_x_pre", "g_mem", "g_x_post", "g_ffn_pre", "g_ffn_post"):
        shared[n] = f(inputs[n]).reshape(1, D)
    for n in ("w_in", "w_out", "w_xq", "w_xkv", "w_xo", "w_gate", "w_up", "w_down"):
        shared[n] = f(inputs[n])
    maps = []
    for b in range(8):
        m = dict(shared)
        m["x"] = x[b]
        m["mem"] = mem[b]
        m["pos"] = np.ascontiguousarray(pos[b].reshape(NT, 128).T)
        maps.append(m)
    return maps


def kernel(**inputs):
    nc = build_nc()
    maps = make_in_maps(inputs)
    res = run_bass_kernel_spmd(nc, maps, core_ids=list(range(8)))
    out = np.stack([np.asarray(r["out"], dtype=np.float32) for r in res.results], axis=0)
    return out
```

```python
import math
import numpy as np
import concourse.bass as bass
import concourse.mybir as mybir
from concourse.bass_utils import run_bass_kernel_spmd

F32 = mybir.dt.float32
BF16 = mybir.dt.bfloat16
I32 = mybir.dt.int32
AF = mybir.ActivationFunctionType
ALU = mybir.AluOpType

S = 2048
D = 1024
NT = 16
KC = 8
DFF = 2816
NFC = 22
MEM = 256
EPS = 1e-6
LAM_INIT = 0.8 - 0.6 * math.exp(-0.3 * 0)
ROPE_THETA = 500000.0
TWO_PI = 2.0 * math.pi


class Op:
    __slots__ = ("idx", "eng", "fn", "deps", "dma", "sig", "dval")

    def __init__(self, idx, eng, fn, deps, dma):
        self.idx = idx
        self.eng = eng
        self.fn = fn
        self.deps = deps
        self.dma = dma
        self.sig = None
        self.dval = None


class Sched:
    ENGS = ("pe", "act", "dve", "pool", "sp")

    def __init__(self):
        self.ops = []
        self.lastw = {}
        self.readers = {}
        self.group_of = {}
        self.final_slots = []

    def add(self, eng, fn, reads=(), writes=(), dma=None):
        idx = len(self.ops)
        rk = [k for k in reads if k[0] != "ps"]
        wk = list(writes) + [k for k in reads if k[0] == "ps" and k not in writes]
        for k in list(rk) + list(wk):
            g = self.group_of.get(k[0])
            if g is not None and (g,) not in rk:
                rk.append((g,))
        deps = set()
        for k in rk:
            w = self.lastw.get(k)
            if w is not None:
                deps.add(w)
        for k in wk:
            w = self.lastw.get(k)
            if w is not None:
                deps.add(w)
            r = self.readers.get(k)
            if r:
                deps.update(r.values())
        deps.discard(idx)
        rkey = ("d", idx) if dma is not None else eng
        for k in rk:
            self.readers.setdefault(k, {})[rkey] = idx
        for k in wk:
            self.lastw[k] = idx
            self.readers[k] = {}
        op = Op(idx, eng, fn, deps, dma)
        self.ops.append(op)
        return op

    def emit(self, nc, block, eng_sem, dma_sem):
        ops = self.ops
        need_sig = set()
        for op in ops:
            for d in op.deps:
                dop = ops[d]
                if dop.dma is not None:
                    continue
                if dop.eng == "pe" and op.eng == "pe" and op.dma is None:
                    continue
                need_sig.add(d)
        cnt = {e: 0 for e in self.ENGS}
        tot = {}
        for op in ops:
            if op.dma is not None:
                tot[op.dma] = tot.get(op.dma, 0) + 16
                op.dval = tot[op.dma]
            elif op.idx in need_sig:
                cnt[op.eng] += 1
                op.sig = cnt[op.eng]
        self.stats = dict(cnt)
        by_eng = {e: [op for op in ops if op.eng == e] for e in self.ENGS}

        def body_for(eng):
            def body(e):
                waited = {}
                for op in by_eng[eng]:
                    need = {}
                    for d in op.deps:
                        dop = ops[d]
                        if dop.dma is not None:
                            key = ("d", dop.dma)
                            val = dop.dval
                        else:
                            if dop.eng == "pe" and eng == "pe" and op.dma is None:
                                continue
                            key = ("c", dop.eng)
                            val = dop.sig
                        if val > need.get(key, 0):
                            need[key] = val
                    for key, val in need.items():
                        if waited.get(key, 0) >= val:
                            continue
                        waited[key] = val
                        sem = dma_sem[key[1]] if key[0] == "d" else eng_sem[key[1]]
                        e.wait_ge(sem, val)
                    ins = op.fn(e)
                    if op.dma is not None:
                        ins.then_inc(dma_sem[op.dma], 16)
                    elif op.idx in need_sig:
                        ins.then_inc(eng_sem[eng], 1)
                if eng == "sp":
                    for slot in self.final_slots:
                        e.wait_ge(dma_sem[slot], tot[slot])
            return body

        block.tensor(body_for("pe"))
        block.scalar(body_for("act"))
        block.vector(body_for("dve"))
        block.gpsimd(body_for("pool"))
        block.sync(body_for("sp"))


class Arena:
    def __init__(self, t, nwords):
        self.t = t
        self.n = nwords
        self.off = 0

    def reset(self):
        self.off = 0

    def alloc(self, shape, dtype):
        n = 1
        for s in shape:
            n *= s
        esz = 4 if dtype in (F32, I32) else 2
        words = (n * esz + 3) // 4
        assert self.off + words <= self.n, (self.off, words, self.n)
        ap = self.t[:, self.off:self.off + words]
        self.off += words
        if dtype != F32:
            ap = ap.bitcast(dtype)
        ap = ap[:, 0:n]
        if len(shape) == 2:
            return ap.rearrange("p (a b) -> p a b", a=shape[0])
        if len(shape) == 3:
            return ap.rearrange("p (a b c) -> p a b c", a=shape[0], b=shape[1])
        return ap


def build_nc(upto="all", dbg_words=0):
    nc = bass.Bass("TRN2", target_bir_lowering=False)

    def din(name, shape, dt=F32):
        return nc.dram_tensor(name, list(shape), dt, kind="ExternalInput").ap()

    x_d = din("x", [S, D])
    mem_d = din("mem", [MEM, D])
    pos_d = din("pos", [128, NT], I32)
    g_d = {n: din(n, [1, D]) for n in ("g_mix_pre", "g_mix_post", "g_x_pre", "g_mem", "g_x_post", "g_ffn_pre", "g_ffn_post")}
    lam_d = din("lamv", [1, 256])
    gsub_d = din("g_subln", [1, 128])
    wpool_d = din("w_pool", [128, 4, 128])
    pscale_d = din("pool_scale", [128, 4])
    w_in_d = din("w_in", [D, 2048])
    w_out_d = din("w_out", [D, D])
    w_xq_d = din("w_xq", [D, D])
    w_xkv_d = din("w_xkv", [D, 2048])
    w_xo_d = din("w_xo", [D, D])
    w_gate_d = din("w_gate", [D, DFF])
    w_up_d = din("w_up", [D, DFF])
    w_down_d = din("w_down", [DFF, D])
    out_d = nc.dram_tensor("out", [S, D], F32, kind="ExternalOutput").ap()
    wg_s = nc.dram_tensor("wg_s", [11, 128, KC * 256], BF16, kind="Internal").ap()
    wu_s = nc.dram_tensor("wu_s", [11, 128, KC * 256], BF16, kind="Internal").ap()
    wd_s = nc.dram_tensor("wd_s", [128, NFC * D], BF16, kind="Internal").ap()
    dbg_d = None
    if dbg_words:
        dbg_d = nc.dram_tensor("dbg", [128, dbg_words], F32, kind="ExternalOutput").ap()

    R2W = 25100
    from contextlib import ExitStack
    with ExitStack() as es:
        def sb(name, shape, dt):
            return es.enter_context(nc.sbuf_tensor(name, list(shape), dt))

        xa_t = sb("xa", [128, 16384], F32)
        actT = sb("actT", [128, KC, S], BF16)
        r2_t = sb("r2", [128, R2W], F32)
        ident = sb("ident", [128, 128], BF16)
        cc = sb("cc", [128, NT, 16], F32)
        ns = sb("ns", [128, NT, 16], F32)
        gb = sb("gb", [128, D], F32)
        gsub8 = sb("gsub8", [128, 128], F32)
        rc = sb("rc", [128, 4, 16], F32)
        pscale = sb("pscale", [128, 4], F32)
        htmp = sb("htmp", [128, 2, D], BF16)
        small = sb("small", [128, 224], F32)
        posi = sb("posi", [128, NT], I32)
        ki = sb("ki", [128, NT * 16], I32)
        ps = es.enter_context(nc.psum_tensor("ps", [128, 8, 512], F32))

        sm_off = [0]

        def sm(n):
            a = small[:, sm_off[0]:sm_off[0] + n]
            sm_off[0] += n
            assert sm_off[0] <= 224
            return a

        warm = sm(2)
        neghalf = sm(1)
        invt = sm(16)
        neg_lam = sm(1)
        lam_t = sm(4)
        ss_n = sm(16)
        vv_n = sm(16)
        rstd_n = sm(16)
        ss_p = sm(16)
        vv_p = sm(16)
        rstd_p = sm(16)
        ss_d = sm(4)
        vv_d = sm(4)
        rstd_d = sm(4)
        rr9 = sm(9)
        rr9b = sm(9)
        ss_d2 = sm(4)
        vv_d2 = sm(4)
        rstd_d2 = sm(4)
        rr2 = sm(4)
        posf = sm(16)
        t16 = sm(16)
        iot = sm(16)

        X = xa_t[:, :].rearrange("p (a b) -> p a b", a=NT)
        XA = Arena(xa_t, 16384)
        R2 = Arena(r2_t, R2W)

        def psb(bank):
            return ps[:, bank, :].bitcast(BF16).rearrange("p (a b) -> p a b", a=8)

        sch = Sched()
        sch.group_of.update({})
        sems = {}
        dma_slots = []

        def slot(name):
            if name not in dma_slots:
                dma_slots.append(name)
            return name

        bank_rr = [0]

        def next_bank():
            b = bank_rr[0]
            bank_rr[0] = (b + 1) % 8
            return b

        def next_pair():
            b = bank_rr[0]
            if b % 2:
                b = (b + 1) % 8
            bank_rr[0] = (b + 2) % 8
            return b

        ev_rr = [0]

        def alt_eng():
            ev_rr[0] ^= 1
            return "act" if ev_rr[0] else "dve"

        def K(name, *idx):
            return (name,) + tuple(idx)

        def actT_keys(kcs, tts):
            return [("actT", kc, tt) for kc in kcs for tt in tts]

        def mm(out_ap, pairs, reads, writes, first_start=True, last_stop=True, skip=False):
            def fn(e):
                ins = None
                n = len(pairs)
                for i, (l, r) in enumerate(pairs):
                    ins = e.matmul(out_ap, l, r, start=(first_start and i == 0), stop=(last_stop and i == n - 1),
                                   skip_group_check=skip)
                return ins
            return sch.add("pe", fn, reads=reads, writes=writes)

        def evac(eng, out_ap, in_ap, reads, writes, scale=None):
            if eng == "act":
                if scale is None:
                    fn = lambda e: e.activation(out=out_ap, in_=in_ap, func=AF.Copy)
                else:
                    fn = lambda e: e.activation(out=out_ap, in_=in_ap, func=AF.Copy, scale=scale)
            else:
                assert scale is None
                fn = lambda e: e.tensor_copy(out=out_ap, in_=in_ap)
            return sch.add(eng, fn, reads=reads, writes=writes)

        def _dma_load(queue, slotname, out_ap, in_ap, writes, reads=()):
            return dma_load(queue, slotname, out_ap, in_ap, writes, reads)

        def dma_load(queue, slotname, out_ap, in_ap, writes, reads=()):
            slot(slotname)
            return sch.add(queue, lambda e: e.dma_start(out=out_ap, in_=in_ap), reads=reads, writes=writes, dma=slotname)

        def load_gb(name):
            dma_load("sp", "gb", gb[:, :], g_d[name][0:1, :].to_broadcast([128, D]), writes=[K("gb")])

        def rstd_chain(ss_ap, vv_ap, rstd_ap, n, key):
            sch.add("pool", lambda e: e.tensor_scalar(out=vv_ap, in0=ss_ap, scalar1=1.0 / n, scalar2=EPS, op0=ALU.mult, op1=ALU.add),
                    reads=[K("ss", key)], writes=[K("vv", key)])
            nh = neghalf if vv_ap.shape[1] == 1 else neghalf.to_broadcast([128, vv_ap.shape[1]])
            sch.add("pool", lambda e: e.tensor_tensor(out=rstd_ap, in0=vv_ap, in1=nh, op=ALU.pow),
                    reads=[K("vv", key), K("neghalf")], writes=[K("rstd", key)])

        bar = sm(1)

        def phase_barrier():
            sch.add("pool", lambda e: e.memset(bar, 0.0), writes=[K("R2")])

        for nm in ("poT", "wsl", "xin", "U", "T", "pl", "qr", "rt1", "rt2", "wpool", "ytmp", "lamb", "xqT", "wsl2", "KcT", "Vc", "memT",
                   "mem_in", "PT2", "xo_tm", "ytmp2", "wd", "hidT", "gsl", "usl", "sg", "ytmp3", "maskc", "accS"):
            sch.group_of[nm] = "R2"

        class Pipe:
            def __init__(self, depth):
                self.depth = depth
                self.q = []

            def push(self, fn):
                self.q.append(fn)
                if len(self.q) > self.depth:
                    self.q.pop(0)()

            def flush(self):
                while self.q:
                    self.q.pop(0)()

        def norm_stats(tt, src_ap, src_keys, junk, junkk):
            sk = ("n", tt)
            ssa, vva, rsa = ss_n[:, tt:tt + 1], vv_n[:, tt:tt + 1], rstd_n[:, tt:tt + 1]
            sch.add("act", lambda e: e.activation(out=junk, in_=src_ap, func=AF.Square, accum_out=ssa),
                    reads=src_keys, writes=[junkk, K("ss", sk)])
            rstd_chain(ssa, vva, rsa, D, sk)

        def norm_scale(tt, src_ap, src_keys, dst_ap, dst_keys, ev=None):
            hs = htmp[:, tt % 2, :]
            hk = K("htmp", tt % 2)
            sk = ("n", tt)
            rsa = rstd_n[:, tt:tt + 1]
            sch.add("dve", lambda e: e.scalar_tensor_tensor(out=hs, in0=src_ap, scalar=rsa, in1=gb[:, :], op0=ALU.mult, op1=ALU.mult),
                    reads=list(src_keys) + [K("rstd", sk), K("gb")], writes=[hk])

            def back():
                bank = next_bank()
                pb = psb(bank)

                def tfn(e):
                    ins = None
                    for kc in range(KC):
                        ins = e.transpose(out=pb[:, kc, :], in_=hs[:, kc * 128:(kc + 1) * 128], identity=ident[:, :])
                    return ins
                sch.add("pe", tfn, reads=[hk, K("ident")], writes=[K("ps", bank)])
                evac(ev if ev is not None else alt_eng(), dst_ap, pb[:, :, :], reads=[K("ps", bank)], writes=dst_keys)
            return back

        def norm_T(tag, tt, src_ap, src_keys, dst_ap, dst_keys, junk=None, junkk=None, ev=None):
            norm_stats(tt, src_ap, src_keys, junk, junkk)
            return norm_scale(tt, src_ap, src_keys, dst_ap, dst_keys, ev=ev)

        def norm_phase(jbufs, jkey):
            for tt in range(NT):
                norm_stats(tt, X[:, tt, :], [K("X", tt)], jbufs[tt % 2], K(jkey, tt % 2))
            pipe = Pipe(1)
            for tt in range(NT):
                pipe.push(norm_scale(tt, X[:, tt, :], [K("X", tt)], actT[:, :, tt * 128:(tt + 1) * 128], actT_keys(range(KC), [tt]), ev="act"))
            pipe.flush()

        pn_cnt = [0]

        def post_norm_res(tag, tt, b0, resid_ap, resid_keys, out_ap, out_keys, ytmp_ap, ytk, after=None):
            sk = ("p", tt)
            y = ps[:, b0:b0 + 2, :]
            yk = [K("ps", b0), K("ps", b0 + 1)]
            ssa, vva, rsa = ss_p[:, tt:tt + 1], vv_p[:, tt:tt + 1], rstd_p[:, tt:tt + 1]
            yt3 = ytmp_ap.rearrange("p (a b) -> p a b", a=2)
            hs = htmp[:, tt % 2, :].rearrange("p (a b) -> p a b", a=2)
            sch.add("act", lambda e: e.activation(out=hs, in_=y, func=AF.Square, accum_out=ssa),
                    reads=yk, writes=[K("htmp", tt % 2), K("ss", sk)])
            rstd_chain(ssa, vva, rsa, D, sk)
            gb3 = gb[:, :].rearrange("p (a b) -> p a b", a=2)
            sch.add("dve", lambda e: e.scalar_tensor_tensor(out=yt3, in0=y, scalar=rsa, in1=gb3, op0=ALU.mult, op1=ALU.mult),
                    reads=yk + [K("rstd", sk), K("gb")], writes=[ytk])
            pn_cnt[0] += 1
            aeng = "dve" if (pn_cnt[0] % 2 or (tag == "p3" and tt == NT - 1) or (tag == "p2" and tt >= NT - 2)) else "pool"

            def back():
                sch.add(aeng, lambda e: e.tensor_tensor(out=out_ap, in0=ytmp_ap, in1=resid_ap, op=ALU.add),
                        reads=[ytk] + list(resid_keys), writes=out_keys)
                if after is not None:
                    after()
            return back

        def w_cols(w_ap, c0, c1):
            return w_ap.rearrange("(kc p) n -> p kc n", p=128)[:, :, c0:c1]

        def load_wslab(slotname, dst_ap, w_ap, c0, c1, key):
            dma_load("pool", slotname, dst_ap, w_cols(w_ap, c0, c1), writes=[key])

        sch.add("pool", lambda e: e.memset(ident[:, :], 0.0), writes=[K("ident")])
        sch.add("pool", lambda e: e.affine_select(out=ident[:, :], in_=ident[:, :], pattern=[[-1, 128]], compare_op=ALU.not_equal,
                                                   fill=1.0, base=0, channel_multiplier=1), reads=[K("ident")], writes=[K("ident")])
        sch.add("pool", lambda e: e.memset(warm, -0.5), writes=[K("warm")])
        sch.add("pool", lambda e: e.memset(neghalf, -0.5), writes=[K("neghalf")])
        for i in range(8):
            invf = ROPE_THETA ** (-(2.0 * i) / 16.0)
            sch.add("pool", (lambda i, invf: lambda e: e.memset(invt.rearrange("p (a b) -> p a b", a=2)[:, :, i], invf))(i, invf), reads=[K("invt")], writes=[K("invt")])
        R2.reset()
        poT = R2.alloc([4, S], BF16)
        lamb = R2.alloc([1, 256], F32)[:, 0, :]

        setup_q = []

        def drain_setup(n):
            for _ in range(n):
                if setup_q:
                    setup_q.pop(0)()

        class _Q:
            @staticmethod
            def add(*a, **k):
                setup_q.append(lambda: sch.add(*a, **k))

        def setup_tables():
          sch = _Q
          dma_load = lambda *a, **k: setup_q.append(lambda: _dma_load(*a, **k))
          dma_load("sp", "posi", posi[:, :], pos_d[:, :], writes=[K("posi")])
          sch.add("dve", lambda e: e.tensor_copy(out=posf, in_=posi[:, :]), reads=[K("posi")], writes=[K("posf")])
          ang = cc
          sch.add("dve", lambda e: e.tensor_tensor(out=ns[:, :, :], in0=posf.unsqueeze(2).to_broadcast([128, NT, 16]),
                                                   in1=invt.unsqueeze(1).to_broadcast([128, NT, 16]), op=ALU.mult),
                  reads=[K("posf"), K("invt")], writes=[K("ns")])
          sch.add("dve", lambda e: e.tensor_scalar(out=ns[:, :, 8:16], in0=ns[:, :, 8:16], scalar1=math.pi / 2, scalar2=None, op0=ALU.add),
                  reads=[K("ns")], writes=[K("ns")])
          nsf = ns[:, :, :].rearrange("p a b -> p (a b)")
          ccf = cc[:, :, :].rearrange("p a b -> p (a b)")
          sch.add("dve", lambda e: e.tensor_scalar(out=ccf, in0=nsf, scalar1=1.0 / TWO_PI, scalar2=None, op0=ALU.mult), reads=[K("ns")], writes=[K("cc")])
          sch.add("dve", lambda e: e.tensor_copy(out=ki[:, :], in_=ccf), reads=[K("cc")], writes=[K("ki")])
          sch.add("dve", lambda e: e.tensor_copy(out=ccf, in_=ki[:, :]), reads=[K("ki")], writes=[K("cc")])
          C1 = 6.28125
          C2 = TWO_PI - C1
          sch.add("dve", lambda e: e.scalar_tensor_tensor(out=nsf, in0=ccf, scalar=-C1, in1=nsf, op0=ALU.mult, op1=ALU.add),
                  reads=[K("cc"), K("ns")], writes=[K("ns")])
          sch.add("dve", lambda e: e.scalar_tensor_tensor(out=nsf, in0=ccf, scalar=-C2, in1=nsf, op0=ALU.mult, op1=ALU.add),
                  reads=[K("cc"), K("ns")], writes=[K("ns")])
          sch.add("dve", lambda e: e.tensor_scalar(out=ccf, in0=nsf, scalar1=math.pi, scalar2=None, op0=ALU.is_gt), reads=[K("ns")], writes=[K("cc")])
          sch.add("dve", lambda e: e.scalar_tensor_tensor(out=nsf, in0=ccf, scalar=-TWO_PI, in1=nsf, op0=ALU.mult, op1=ALU.add),
                  reads=[K("cc"), K("ns")], writes=[K("ns")])
          sch.add("dve", lambda e: e.tensor_scalar(out=ccf, in0=nsf, scalar1=-math.pi, scalar2=None, op0=ALU.is_lt), reads=[K("ns")], writes=[K("cc")])
          sch.add("dve", lambda e: e.scalar_tensor_tensor(out=nsf, in0=ccf, scalar=TWO_PI, in1=nsf, op0=ALU.mult, op1=ALU.add),
                  reads=[K("cc"), K("ns")], writes=[K("ns")])
          sch.add("dve", lambda e: e.tensor_scalar(out=nsf, in0=nsf, scalar1=math.pi, scalar2=-math.pi, op0=ALU.min, op1=ALU.max), reads=[K("ns")], writes=[K("ns")])
          sch.add("act", lambda e: e.activation(out=ccf, in_=nsf, func=AF.Sin), reads=[K("ns")], writes=[K("cc")])
          sch.add("dve", lambda e: e.tensor_scalar(out=ns[:, :, 0:8], in0=cc[:, :, 0:8], scalar1=-1.0, scalar2=None, op0=ALU.mult), reads=[K("cc")], writes=[K("ns")])
          sch.add("dve", lambda e: e.tensor_copy(out=ns[:, :, 8:16], in_=cc[:, :, 0:8]), reads=[K("cc"), K("ns")], writes=[K("ns")])
          sch.add("dve", lambda e: e.tensor_copy(out=cc[:, :, 0:8], in_=cc[:, :, 8:16]), reads=[K("cc"), K("ns")], writes=[K("cc")])
          dma_load("sp", "lamb", lamb, lam_d[0:1, :].to_broadcast([128, 256]), writes=[K("lamb")])
          for i in range(2):
              sch.add("dve", (lambda i: lambda e: e.tensor_tensor(out=lamb[:, 128 * i:128 * i + 64], in0=lamb[:, 128 * i:128 * i + 64], in1=lamb[:, 128 * i + 64:128 * i + 128], op=ALU.mult))(i),
                      reads=[K("lamb")], writes=[K("lamb")])
              sch.add("dve", (lambda i: lambda e: e.tensor_reduce(out=lam_t[:, i:i + 1], in_=lamb[:, 128 * i:128 * i + 64], axis=mybir.AxisListType.X, op=ALU.add))(i),
                      reads=[K("lamb")], writes=[K("lam_t")])
          sch.add("act", lambda e: e.activation(out=lam_t[:, 2:4], in_=lam_t[:, 0:2], func=AF.Exp), reads=[K("lam_t")], writes=[K("lam_t")])
          sch.add("dve", lambda e: e.scalar_tensor_tensor(out=neg_lam, in0=lam_t[:, 3:4], scalar=-LAM_INIT, in1=lam_t[:, 2:3], op0=ALU.add, op1=ALU.subtract),
                  reads=[K("lam_t")], writes=[K("neg_lam")])
          dma_load("sp", "gsub", gsub8[:, :], gsub_d[0:1, :].to_broadcast([128, 128]), writes=[K("gsub")])
          sch.add("dve", lambda e: e.tensor_scalar(out=gsub8[:, :], in0=gsub8[:, :], scalar1=1.0 - LAM_INIT, scalar2=None, op0=ALU.mult), reads=[K("gsub")], writes=[K("gsub")])
          sch.add("pool", lambda e: e.iota(out=ki[:, 0:16], pattern=[[1, 16]], base=1, channel_multiplier=0), reads=[K("ki")], writes=[K("ki")])
          sch.add("dve", lambda e: e.tensor_copy(out=iot, in_=ki[:, 0:16]), reads=[K("ki")], writes=[K("iot")])
          for g in range(4):
              sch.add("dve", (lambda g: lambda e: e.tensor_scalar(out=rc[:, g, :], in0=iot, scalar1=float(2 << g), scalar2=None, op0=ALU.min))(g), reads=[K("iot")], writes=[K("rc")])
          sch.add("dve", lambda e: e.reciprocal(out=rc[:, :, :], in_=rc[:, :, :]), reads=[K("rc")], writes=[K("rc")])
          dma_load("sp", "pscale", pscale[:, :], pscale_d[:, :], writes=[K("pscale")])

        XA.reset()
        wsl = [R2.alloc([KC, 512], BF16) for _ in range(3)]
        xin = [R2.alloc([1, D], F32)[:, 0, :] for _ in range(4)]
        Ub = [R2.alloc([1, 528], F32)[:, 0, :] for _ in range(2)]
        halo = R2.alloc([4, 16], F32)
        Tb = {"pool": [R2.alloc([1, 528], F32)[:, 0, :] for _ in range(2)], "dve": [R2.alloc([1, 528], F32)[:, 0, :] for _ in range(2)]}
        plb = [R2.alloc([1, 512], BF16)[:, 0, :] for _ in range(3)]
        qrb = [R2.alloc([8, 64], BF16) for _ in range(3)]
        rt1 = [R2.alloc([8, 16], F32) for _ in range(3)]
        rt2 = [R2.alloc([8, 16], F32) for _ in range(3)]
        wpool = R2.alloc([4, 128], BF16)
        ytmp = [R2.alloc([1, D], F32)[:, 0, :] for _ in range(2)]
        QT = XA.alloc([4, S], BF16)
        KT = XA.alloc([4, S], BF16)
        Vflat = XA.alloc([NT * 4, 130], BF16)
        V = Vflat.rearrange("p (a b) c -> p a b c", a=NT)
        PT = [XA.alloc([2, 512], BF16) for _ in range(3)]
        MA_lo = R2.alloc([1, 128], BF16)[:, 0, :]
        MA_hi = R2.alloc([1, 128], BF16)[:, 0, :]
        ID_lo = R2.alloc([1, 128], BF16)[:, 0, :]
        ID_hi = R2.alloc([1, 128], BF16)[:, 0, :]
        NEG = -30000.0
        mk = [K("maskc")]
        for (t, lo, hi, base) in ((MA_lo, 0, 64, 0), (MA_lo, 64, 128, 0), (MA_hi, 0, 64, -64), (MA_hi, 64, 128, -64)):
            sch.add("pool", (lambda t, lo, hi: lambda e: e.memset(t[lo:hi, :], 0.0))(t, lo, hi), reads=mk, writes=mk)
            sch.add("pool", (lambda t, lo, hi, base: lambda e: e.affine_select(out=t[lo:hi, :], in_=t[lo:hi, :], pattern=[[-1, 128]], compare_op=ALU.is_ge,
                                                                                   fill=NEG, base=-base, channel_multiplier=1))(t, lo, hi, base), reads=mk, writes=mk)
        for (t, lo, hi, base) in ((ID_lo, 0, 64, 0), (ID_lo, 64, 128, 0), (ID_hi, 0, 64, -64), (ID_hi, 64, 128, -64)):
            sch.add("pool", (lambda t, lo, hi: lambda e: e.memset(t[lo:hi, :], 0.0))(t, lo, hi), reads=mk, writes=mk)
            sch.add("pool", (lambda t, lo, hi, base: lambda e: e.affine_select(out=t[lo:hi, :], in_=t[lo:hi, :], pattern=[[-1, 128]], compare_op=ALU.not_equal,
                                                                                   fill=1.0, base=-base, channel_multiplier=1))(t, lo, hi, base), reads=mk, writes=mk)
        finset = [dict(T9=XA.alloc([9, 128], F32), da4=XA.alloc([4, 128], F32), dabb4=XA.alloc([4, 128], BF16), ss=ss_d, vv=vv_d, rstd=rstd_d, rr9=rr9),
                  dict(T9=R2.alloc([9, 128], F32), da4=R2.alloc([4, 128], F32), dabb4=R2.alloc([4, 128], BF16), ss=ss_d2, vv=vv_d2, rstd=rstd_d2, rr9=rr9b)]
        XRK = K("XR")
        for nm in ("QT", "KT", "V", "PT", "T9_0", "da4_0", "dabb4_0"):
            sch.group_of[nm] = "XR"
        for nm in ("T9_1", "da4_1", "dabb4_1"):
            sch.group_of[nm] = "R2"

        sch.add("pool", lambda e: e.memset(Vflat[:, :, 128:129], 1.0), writes=[K("V", "ones")])
        load_gb("g_mix_pre")
        def load_slab1(si, w_ap, c0, c1):
            load_wslab("wsl%d" % si, wsl[si][:, :, :], w_ap, c0, c1, K("wsl", si))

        def start_loads(it):
            if it == 0:
                load_slab1(1, w_in_d, 0, 512)
            elif it == 1:
                load_slab1(0, w_in_d, 1024, 1536)
            elif it == 2:
                load_slab1(2, w_in_d, 512, 1024)
            elif it == 3:
                dma_load("pool", "wpool", wpool[:, :, :], wpool_d[:, :, :], writes=[K("wpool")])
            elif it == 5:
                for ce in ("pool", "dve"):
                    for ti in range(2):
                        sch.add("pool", (lambda t: lambda e: e.memset(t[:, :], 0.0))(Tb[ce][ti]), writes=[K("T", ce, ti)])

        sch.add("act", lambda e: e.activation(out=warm[:, 1:2], in_=warm[:, 0:1], func=AF.Square), reads=[K("warm")], writes=[K("warm")])
        setup_tables()
        conv_q = []
        cv_n = [0]

        def cv_chain():
            i = cv_n[0]
            cv_n[0] += 1
            return ([K("cvchain", i - 8)] if i >= 8 else []), [K("cvchain", i)]

        slab_plan2 = [(w_xkv_d, 0, 512), (w_xkv_d, 512, 1024), (w_xkv_d, 1024, 1536), (w_xkv_d, 1536, 2048), (w_xq_d, 0, 512), (w_xq_d, 512, 1024),
                      (w_xo_d, 0, 512), (w_xo_d, 512, 1024)]
        for s_ in range(11):
            for (nm, w_d, scr) in (("wg", w_gate_d, wg_s), ("wu", w_up_d, wu_s)):
                def cv(s_=s_, nm=nm, w_d=w_d, scr=scr):
                    sn = slot("cv_%s%d" % (nm, s_))
                    rd, wr = cv_chain()
                    sch.add("pool", lambda e: e.dma_start(out=scr[s_, :, :].rearrange("p (a b) -> p a b", a=KC), in_=w_cols(w_d, s_ * 256, (s_ + 1) * 256)),
                            reads=rd, writes=[K(nm + "scr", s_)] + wr, dma=sn)
                conv_q.append(cv)
        for hf in range(4):
            def cvd(hf=hf):
                sn = slot("cv_wd%d" % hf)
                f0, f1 = (0, 6, 12, 17, 22)[hf], (0, 6, 12, 17, 22)[hf + 1]
                src = w_down_d.rearrange("(fc p) n -> p fc n", p=128)[:, f0:f1, :]
                dst = wd_s.rearrange("p (a b) -> p a b", a=NFC)[:, f0:f1, :]
                rd, wr = cv_chain()
                sch.add("pool", lambda e: e.dma_start(out=dst, in_=src), reads=rd, writes=[K("wdscr", hf)] + wr, dma=sn)
            conv_q.append(cvd)

        def drain_conv(n):
            for _ in range(n):
                if conv_q:
                    conv_q.pop(0)()

        qk_pipe = Pipe(2)
        qcount = [0]

        def proj_tile(ci, tt, sl):
            bank = next_bank()
            mm(ps[:, bank, :], [(actT[:, kc, tt * 128:(tt + 1) * 128], wsl[sl][:, kc, :]) for kc in range(KC)],
               reads=[K("wsl", sl)] + actT_keys(range(KC), [tt]), writes=[K("ps", bank)])
            if ci == 1:
                drain_conv(1)
            if ci == 2:
                evac("act", V[:, tt, :, 0:128], ps[:, bank, :].rearrange("p (a b) -> p a b", a=4), reads=[K("ps", bank)], writes=[K("V", tt)])
                return
            psv = ps[:, bank, :].rearrange("p (a b) -> p a b", a=8)
            qs = qcount[0] % 3
            qcount[0] += 1
            qr = qrb[qs]
            qk = K("qr", qs)
            import os
            if "norope" in os.environ.get("KSKIP", ""):
                sch.add("act", (lambda qr, psv: lambda e: e.activation(out=qr[:, :, :], in_=psv[:, :, :], func=AF.Copy))(qr, psv),
                        reads=[K("ps", bank)], writes=[qk])
            else:
                sch.add("act", (lambda qr, psv: lambda e: e.activation(out=qr[:, :, 16:64], in_=psv[:, :, 16:64], func=AF.Copy))(qr, psv),
                        reads=[K("ps", bank)], writes=[qk])
                ccb = cc[:, tt:tt + 1, :].to_broadcast([128, 8, 16])
                nsa = ns[:, tt:tt + 1, 0:8].to_broadcast([128, 8, 8])
                nsb = ns[:, tt:tt + 1, 8:16].to_broadcast([128, 8, 8])
                a1, a2 = rt1[qs], rt2[qs]
                sch.add("dve", (lambda a1, psv, ccb: lambda e: e.tensor_tensor(out=a1[:, :, :], in0=psv[:, :, 0:16], in1=ccb, op=ALU.mult))(a1, psv, ccb),
                        reads=[K("ps", bank), K("cc")], writes=[K("rt1", qs)])
                sch.add("dve", (lambda a2, psv, nsa: lambda e: e.tensor_tensor(out=a2[:, :, 0:8], in0=psv[:, :, 8:16], in1=nsa, op=ALU.mult))(a2, psv, nsa),
                        reads=[K("ps", bank), K("ns")], writes=[K("rt2", qs)])
                sch.add("dve", (lambda a2, psv, nsb: lambda e: e.tensor_tensor(out=a2[:, :, 8:16], in0=psv[:, :, 0:8], in1=nsb, op=ALU.mult))(a2, psv, nsb),
                        reads=[K("ps", bank), K("ns")], writes=[K("rt2", qs)])
                sch.add("dve", (lambda qr, a1, a2: lambda e: e.tensor_tensor(out=qr[:, :, 0:16], in0=a1[:, :, :], in1=a2[:, :, :], op=ALU.add))(qr, a1, a2),
                        reads=[K("rt1", qs), K("rt2", qs)], writes=[qk])
            def back_qk(qr=qr, qk=qk, ci=ci, tt=tt):
                bank2 = next_bank()
                pb = psb(bank2)
                qrf = qr.rearrange("p a b -> p (a b)")

                def tfn(e):
                    ins = None
                    for h in range(4):
                        ins = e.transpose(out=pb[:, h, :], in_=qrf[:, h * 128:(h + 1) * 128], identity=ident[:, :])
                    return ins
                sch.add("pe", tfn, reads=[qk, K("ident")], writes=[K("ps", bank2)])
                dstT = QT if ci == 0 else KT
                evac("act", dstT[:, :, tt * 128:(tt + 1) * 128], pb[:, 0:4, :], reads=[K("ps", bank2)], writes=[K("QT" if ci == 0 else "KT", tt)])
            qk_pipe.push(back_qk)

        pipe = Pipe(1)
        for tt in range(NT):
            xs = xin[tt % 4]
            dma_load("sp", "xin%d" % (tt % 4), xs, x_d[tt * 128:(tt + 1) * 128, :], writes=[K("xin", tt % 4)])
            pipe.push(norm_T("n1", tt, xs, [K("xin", tt % 4)], actT[:, :, tt * 128:(tt + 1) * 128], actT_keys(range(KC), [tt]), junk=ytmp[tt % 2], junkk=K("ytmp", tt % 2), ev="dve"))
            start_loads(tt)
            drain_setup(10)
            if tt >= 4:
                proj_tile(0, tt - 4, 1)
                proj_tile(2, tt - 4, 0)
        pipe.flush()
        drain_setup(1000)
        if upto == "1a":
            return _finish(nc, es, sch, dma_slots, dbg_d, [(actT[:, :, :].rearrange("p a b -> p (a b)"), actT_keys(range(KC), range(NT)), KC * S, BF16)], locals())
        for tt in range(NT - 4, NT):
            proj_tile(0, tt, 1)
        load_slab1(1, w_in_d, 1536, 2048)
        for tt in range(NT - 4, NT):
            proj_tile(2, tt, 0)
        load_slab1(0, w_out_d, 0, 512)
        for tt in range(3):
            proj_tile(1, tt, 2)

        s_u = 1
        ui = 0
        pipe = Pipe(2)
        unit = 0
        unit_order = [(g_, tg_) for (ga, gb_) in ((0, 1), (3, 2)) for tg_ in range(4) for g_ in (ga, gb_)]
        for (g, tg) in unit_order:
            w = 2 << g
            ceng = "pool" if g in (0, 3) else "dve"
            if True:
                bank = next_bank()
                mm(ps[:, bank, :], [(wsl[s_u][:, kc, g * 128:(g + 1) * 128], actT[:, kc, tg * 512:(tg + 1) * 512]) for kc in range(KC)],
                   reads=[K("wsl", s_u)] + actT_keys(range(KC), range(tg * 4, tg * 4 + 4)), writes=[K("ps", bank)])
                us = ui % 2
                ui += 1
                U = Ub[us]
                hal = halo[:, g, :]
                if tg == 0:
                    sch.add("pool", (lambda U: lambda e: e.memset(U[:, 0:16], 0.0))(U), writes=[K("U", us)])
                else:
                    sch.add("pool", (lambda U, hal: lambda e: e.tensor_copy(out=U[:, 0:16], in_=hal))(U, hal), reads=[K("halo", g)], writes=[K("U", us)])
                evac("act", U[:, 16:528], ps[:, bank, :], reads=[K("ps", bank)], writes=[K("U", us)])
                if tg < 3:
                    sch.add("pool", (lambda U, hal: lambda e: e.tensor_copy(out=hal, in_=U[:, 512:528]))(U, hal), reads=[K("U", us)], writes=[K("halo", g)])
                src = U
                srck = K("U", us)
                for lvl in range(g + 1):
                    sh = 1 << lvl
                    dst = Tb[ceng][lvl % 2]
                    dstk = K("T", ceng, lvl % 2)
                    sch.add(ceng, (lambda dst, src, sh: lambda e: e.tensor_tensor(out=dst[:, sh:528], in0=src[:, sh:528], in1=src[:, 0:528 - sh], op=ALU.add))(dst, src, sh),
                            reads=[srck], writes=[dstk])
                    src, srck = dst, dstk
                pls = ui % 3
                pl = plb[pls]
                sch.add("dve", (lambda pl, src, U, w: lambda e: e.scalar_tensor_tensor(out=pl[:, :], in0=src[:, 16:528], scalar=1.0 / w, in1=U[:, 16:528], op0=ALU.mult, op1=ALU.subtract))(pl, src, U, w),
                        reads=[srck, K("U", us)], writes=[K("pl", pls)])
                if tg == 0:
                    sch.add("dve", (lambda src, g: lambda e: e.tensor_tensor(out=t16, in0=src[:, 16:32], in1=rc[:, g, :], op=ALU.mult))(src, g),
                            reads=[srck, K("rc")], writes=[K("t16")])
                    sch.add("dve", (lambda pl, U: lambda e: e.tensor_tensor(out=pl[:, 0:16], in0=t16, in1=U[:, 16:32], op=ALU.subtract))(pl, U),
                            reads=[K("t16"), K("U", us)], writes=[K("pl", pls)])
                def back_pool(g=g, tg=tg, pl=pl, pls=pls):
                    bank2 = next_bank()
                    mm(ps[:, bank2, :], [(wpool[:, g, :], pl[:, :])], reads=[K("wpool"), K("pl", pls)], writes=[K("ps", bank2)])
                    evac("act", poT[:, g, tg * 512:(tg + 1) * 512], ps[:, bank2, :], reads=[K("ps", bank2), K("pscale")], writes=[K("poT", g, tg)], scale=pscale[:, g:g + 1])
                pipe.push(back_pool)
                if unit + 3 < NT:
                    proj_tile(1, unit + 3, 2)
                unit += 1
        pipe.flush()
        qk_pipe.flush()
        load_slab1(1, w_out_d, 512, 1024)
        if upto == "1b":
            return _finish(nc, es, sch, dma_slots, dbg_d, [(poT[:, :, :].rearrange("p a b -> p (a b)"), [K("poT", g, tg) for g in range(4) for tg in range(4)], 4 * S, BF16)], locals())

        if upto == "1c":
            return _finish(nc, es, sch, dma_slots, dbg_d,
                           [(QT[:, :, :].rearrange("p a b -> p (a b)"), [K("QT", tt) for tt in range(NT)], 4 * S, BF16),
                            (KT[:, :, :].rearrange("p a b -> p (a b)"), [K("KT", tt) for tt in range(NT)], 4 * S, BF16),
                            (V[:, :, :, :].rearrange("p a b c -> p (a b c)"), [K("V", tt) for tt in range(NT)] + [K("V", "ones")], NT * 4 * 130, BF16)], locals())

        steps = [(h, G, j) for h in range(4) for G in range(4) for j in range(4 * G + 4)]

        def st_info(step):
            h, G, j = step
            q0 = max(4 * G, j)
            n = (4 * G + 4 - q0) * 128
            return q0, n

        def emit_ST(i):
            h, G, j = steps[i]
            q0, n = st_info(steps[i])
            p = i % 2
            qks = [K("QT", t) for t in range(q0, 4 * G + 4)] + [K("KT", j)]

            diag = j >= 4 * G

            def fn(e):
                ins = None
                for m in range(2):
                    r0, r1 = 64 * m, 64 * m + 64
                    ins = e.matmul(ps[0:128, 2 * p + m, 0:n], KT[r0:r1, h, j * 128:(j + 1) * 128], QT[r0:r1, h, q0 * 128:q0 * 128 + n], start=True, stop=True)
                    if diag:
                        e.matmul(ps[0:128, 2 * p + m, 0:128], MA_lo[r0:r1, :], ID_lo[r0:r1, :], start=False, stop=False, skip_group_check=True)
                        ins = e.matmul(ps[0:128, 2 * p + m, 0:128], MA_hi[r0:r1, :], ID_hi[r0:r1, :], start=False, stop=True, skip_group_check=True)
                return ins
            sch.add("pe", fn, reads=qks + ([K("maskc")] if diag else []), writes=[K("ps", 2 * p), K("ps", 2 * p + 1)])

        pend_q = []
        fin_cnt = [0]

        late_T = []

        def run_pend(force=False, hold_T=False):
            for it in list(pend_q):
                it[0] -= 1
                if it[0] <= 0 or force:
                    pend_q.remove(it)
                    if hold_T and len(it) > 2:
                        late_T.append(it[1])
                    else:
                        it[1]()
        sch.add("dve", lambda e: e.memset(ps[:, 6, 258:387], 1.0), writes=[K("ps", 6)])
        emit_ST(0)
        emit_ST(1)
        for i, (h, G, j) in enumerate(steps):
            q0, n = st_info(steps[i])
            p = i % 2
            pt = PT[i % 3]
            ptk = K("PT", i % 3)
            sch.add("act", (lambda pt, p, n: lambda e: e.activation(out=pt[:, :, 0:n], in_=ps[:, 2 * p:2 * p + 2, 0:n], func=AF.Exp, scale=0.125))(pt, p, n),
                    reads=[K("ps", 2 * p), K("ps", 2 * p + 1)], writes=[ptk])
            if i + 2 < len(steps):
                emit_ST(i + 2)

            def pvfn(e, pt=pt, q0=q0, n=n, G=G, j=j, h=h):
                ins = None
                for qi in range(n // 128):
                    il = q0 + qi - 4 * G
                    for m in range(2):
                        a = il * 2 + m
                        bank = 4 + a // 3
                        off = (a % 3) * 129
                        ins = e.matmul(ps[:, bank, off:off + 129], pt[:, m, qi * 128:(qi + 1) * 128], V[:, j, h, 0:129],
                                       start=(j == 0 and a % 3 == 0), stop=(j == 4 * G + il), skip_group_check=True)
                return ins
            sch.add("pe", pvfn, reads=[ptk, K("V", j), K("V", "ones")], writes=[K("ps", 4), K("ps", 5), K("ps", 6)])
            run_pend()
            if i % 4 == 3:
                drain_conv(1)
            if j == 4 * G + 3:
                fi = fin_cnt[0] % 2
                fin_cnt[0] += 1
                fs = finset[fi]
                T9, da4, dabb4, rr9c, ssd, vvd, rsd = fs["T9"], fs["da4"], fs["dabb4"], fs["rr9"], fs["ss"], fs["vv"], fs["rstd"]
                kT9, kda, kdb, krr = K("T9_%d" % fi), K("da4_%d" % fi), K("dabb4_%d" % fi), K("rr9", fi)
                acck = [K("ps", 4), K("ps", 5), K("ps", 6)]
                acc4 = ps[:, 4:7, 0:387].rearrange("p b (s c) -> p b s c", c=129)
                rr9v = rr9c.rearrange("p (b s) -> p b s", b=3)
                sch.add("dve", (lambda acc4, rr9v: lambda e: e.reciprocal(out=rr9v, in_=acc4[:, :, :, 128]))(acc4, rr9v), reads=acck, writes=[krr])
                rr8 = rr9c[:, 0:8].rearrange("p (a b) -> p a b", b=2)
                sch.add("dve", (lambda rr8: lambda e: e.tensor_scalar(out=rr8[:, :, 1], in0=rr8[:, :, 1], scalar1=neg_lam, scalar2=None, op0=ALU.mult))(rr8),
                        reads=[krr, K("neg_lam")], writes=[krr])
                T9v = T9.rearrange("p (b s) c -> p b s c", b=3)
                wbc = rr9v.unsqueeze(3).to_broadcast([128, 3, 3, 128])
                sch.add("dve", (lambda T9v, acc4, wbc: lambda e: e.tensor_tensor(out=T9v, in0=acc4[:, :, :, 0:128], in1=wbc, op=ALU.mult))(T9v, acc4, wbc),
                        reads=acck + [krr], writes=[kT9])
                T8 = T9[:, 0:8, :].rearrange("p (a b) c -> p a b c", b=2)
                sch.add("dve", (lambda T8, da4: lambda e: e.tensor_tensor(out=da4[:, :, :], in0=T8[:, :, 0, :], in1=T8[:, :, 1, :], op=ALU.add))(T8, da4),
                        reads=[kT9], writes=[kda])
                sq = T9[:, 0:4, :]
                tmpn = T9[:, 4:8, :]

                def fin_part2(sq=sq, tmpn=tmpn, da4=da4, dabb4=dabb4, ssd=ssd, vvd=vvd, rsd=rsd, kda=kda, kT9=kT9, kdb=kdb, fi=fi, h=h, G=G):
                  sch.add("dve", (lambda sq, da4: lambda e: e.tensor_tensor(out=sq, in0=da4[:, :, :], in1=da4[:, :, :], op=ALU.mult))(sq, da4), reads=[kda], writes=[kT9])
                  sch.add("dve", (lambda sq, ssd: lambda e: e.tensor_reduce(out=ssd, in_=sq, axis=mybir.AxisListType.X, op=ALU.add))(sq, ssd), reads=[kT9], writes=[K("ss", ("sub", fi))])
                  rstd_chain(ssd, vvd, rsd, 128, ("sub", fi))

                  def fin_part2b():
                    rbc = rsd.unsqueeze(2).to_broadcast([128, 4, 128])
                    gbc = gsub8[:, :].unsqueeze(1).to_broadcast([128, 4, 128])
                    sch.add("dve", (lambda tmpn, rbc, da4: lambda e: e.tensor_tensor(out=tmpn, in0=da4[:, :, :], in1=rbc, op=ALU.mult))(tmpn, rbc, da4),
                            reads=[kda, K("rstd", ("sub", fi))], writes=[kT9])
                    sch.add("dve", (lambda tmpn, gbc, dabb4: lambda e: e.tensor_tensor(out=dabb4[:, :, :], in0=tmpn, in1=gbc, op=ALU.mult))(tmpn, gbc, dabb4),
                            reads=[kT9, K("gsub")], writes=[kdb])
                    pend_q.append([3, fin_T, "T"])

                  def fin_T(h=h, G=G, dabb4=dabb4, kdb=kdb):
                      pb = psb(7)

                      def tfn2(e):
                          ins = None
                          for il in range(4):
                              ins = e.transpose(out=pb[:, il, :], in_=dabb4[:, il, :], identity=ident[:, :])
                          return ins
                      sch.add("pe", tfn2, reads=[kdb, K("ident")], writes=[K("ps", 7)])
                      evac("dve", actT[:, h, G * 512:(G + 1) * 512].rearrange("p (a b) -> p a b", a=4), pb[:, 0:4, :], reads=[K("ps", 7)],
                           writes=actT_keys([h], range(4 * G, 4 * G + 4)))
                  pend_q.append([1, fin_part2b])
                nxt_short = (i + 1 < len(steps)) and steps[i + 1][1] == 0
                pend_q.append([5 if nxt_short else 1, fin_part2])
        while pend_q:
            run_pend(force=True, hold_T=(upto != "1d"))
        if upto == "1d":
            return _finish(nc, es, sch, dma_slots, dbg_d, [(actT[:, 0:4, :].rearrange("p a b -> p (a b)"), actT_keys(range(4), range(NT)), 4 * S, BF16)], locals())

        load_gb("g_mix_post")
        pipe = Pipe(1)
        for tt in range(NT):
            xs = xin[tt % 4]
            dma_load("sp", "xin%d" % (tt % 4), xs, x_d[tt * 128:(tt + 1) * 128, :], writes=[K("xin", tt % 4)])
            if tt == 3:
                while late_T:
                    late_T.pop(0)()
            b0 = next_pair()
            for cg in range(2):
                pairs = [(actT[:, kc, tt * 128:(tt + 1) * 128], wsl[cg][:, kc, :]) for kc in range(4)] + \
                        [(poT[:, kc, tt * 128:(tt + 1) * 128], wsl[cg][:, 4 + kc, :]) for kc in range(4)]
                mm(ps[:, b0 + cg, :], pairs, reads=[K("wsl", cg)] + actT_keys(range(4), [tt]) + [K("poT", g, tt // 4) for g in range(4)], writes=[K("ps", b0 + cg)])
            pipe.push(post_norm_res("p1", tt, b0, xs, [K("xin", tt % 4)], X[:, tt, :], [K("X", tt), XRK], ytmp[tt % 2], K("ytmp", tt % 2)))
        pipe.flush()
        if upto == "1e":
            return _finish(nc, es, sch, dma_slots, dbg_d, [(X[:, :, :].rearrange("p a b -> p (a b)"), [K("X", tt) for tt in range(NT)], NT * D, F32)], locals())

        R2.reset()
        xqT = R2.alloc([KC, S], BF16)
        wsl2 = [R2.alloc([KC, 512], BF16) for _ in range(2)]
        KcT = R2.alloc([KC, MEM], BF16)
        Vcflat = R2.alloc([2 * 4, 258], BF16)
        Vc = Vcflat.rearrange("p (a b) c -> p a b c", a=2)
        memT = R2.alloc([KC, MEM], BF16)
        mem_in = [R2.alloc([1, D], F32)[:, 0, :] for _ in range(2)]
        PT2 = [R2.alloc([2, 512], BF16) for _ in range(2)]
        xo_tms = [R2.alloc([4, D], BF16) for _ in range(2)]
        ytmp = [R2.alloc([1, D], F32)[:, 0, :] for _ in range(2)]
        phase_barrier()

        def P2(keys):
            return list(keys)

        slab_j = [0]

        def next_slab2():
            i = slab_j[0]
            slab_j[0] += 1
            w_ap, c0, c1 = slab_plan2[i]
            load_wslab("wslb%d" % (i % 2), wsl2[i % 2][:, :, :], w_ap, c0, c1, K("wsl2", i % 2))
            return i % 2

        next_slab2()
        next_slab2()
        sch.add("pool", lambda e: e.memset(Vcflat[:, :, 256:257], 1.0), writes=[K("Vc", "ones")])
        load_gb("g_mem")
        for mt in range(2):
            dma_load("sp", "memin%d" % mt, mem_in[mt], mem_d[mt * 128:(mt + 1) * 128, :], writes=[K("mem_in", mt)])
            norm_T("nm", mt, mem_in[mt], [K("mem_in", mt)], memT[:, :, mt * 128:(mt + 1) * 128], [K("memT", mt)], junk=ytmp[mt % 2], junkk=K("ytmp2", mt % 2))()
        memk = [K("memT", 0), K("memT", 1)]
        load_gb("g_x_pre")
        for tt in range(NT):
            norm_stats(tt, X[:, tt, :], [K("X", tt)], ytmp[tt % 2], K("ytmp2", tt % 2))
        xpipe = Pipe(1)
        xt = [0]

        def x_tile():
            if xt[0] < NT:
                tt = xt[0]
                xt[0] += 1
                xpipe.push(norm_scale(tt, X[:, tt, :], [K("X", tt)], actT[:, :, tt * 128:(tt + 1) * 128], actT_keys(range(KC), [tt]), ev="act"))

        for sl in range(2):
            for ocl in range(4):
                oc = sl * 4 + ocl
                bank = next_bank()
                mm(ps[:, bank, 0:MEM], [(wsl2[sl][:, kc, ocl * 128:(ocl + 1) * 128], memT[:, kc, :]) for kc in range(KC)], reads=[K("wsl2", sl)] + memk, writes=[K("ps", bank)])
                evac(alt_eng(), KcT[:, oc, :], ps[:, bank, 0:MEM], reads=[K("ps", bank)], writes=[K("KcT", oc)])
                x_tile()
            next_slab2()
        for sl in range(2):
            for mt in range(2):
                bank = next_bank()
                mm(ps[:, bank, :], [(memT[:, kc, mt * 128:(mt + 1) * 128], wsl2[sl][:, kc, :]) for kc in range(KC)], reads=[K("wsl2", sl)] + memk, writes=[K("ps", bank)])
                evac(alt_eng(), Vc[:, mt, 2 * sl:2 * sl + 2, 0:256], ps[:, bank, :].rearrange("p (a b) -> p a b", a=2), reads=[K("ps", bank)], writes=[K("Vc", mt, sl)])
                x_tile()
                x_tile()
            next_slab2()
        while xt[0] < NT:
            x_tile()
        xpipe.flush()
        for sl in range(2):
            for ocl in range(4):
                oc = sl * 4 + ocl
                for tg in range(4):
                    bank = next_bank()
                    mm(ps[:, bank, :], [(wsl2[sl][:, kc, ocl * 128:(ocl + 1) * 128], actT[:, kc, tg * 512:(tg + 1) * 512]) for kc in range(KC)],
                       reads=[K("wsl2", sl)] + actT_keys(range(KC), range(4 * tg, 4 * tg + 4)), writes=[K("ps", bank)])
                    evac(alt_eng(), xqT[:, oc, tg * 512:(tg + 1) * 512], ps[:, bank, :], reads=[K("ps", bank)], writes=[K("xqT", oc, tg)])
            next_slab2()
        if upto == "2c":
            return _finish(nc, es, sch, dma_slots, dbg_d,
                           [(xqT[:, :, :].rearrange("p a b -> p (a b)"), [K("xqT", oc, tg) for oc in range(8) for tg in range(4)], KC * S, BF16),
                            (KcT[:, :, :].rearrange("p a b -> p (a b)"), [K("KcT", oc) for oc in range(8)], KC * MEM, BF16),
                            (Vc[:, :, :, :].rearrange("p a b c -> p (a b c)"), [K("Vc", mt, sl) for mt in range(2) for sl in range(2)] + [K("Vc", "ones")], 2 * 4 * 258, BF16)], locals())
        xsteps = [(tg, h) for tg in range(4) for h in range(4)]
        sbank_rr = [0]

        def emit_SC(i, mt):
            tg, h = xsteps[i]
            sbk = (2 * i + mt) % 3
            mm(ps[:, sbk, :], [(KcT[:, 2 * h + dc, mt * 128:(mt + 1) * 128], xqT[:, 2 * h + dc, tg * 512:(tg + 1) * 512]) for dc in range(2)],
               reads=[K("KcT", 2 * h), K("KcT", 2 * h + 1), K("xqT", 2 * h, tg), K("xqT", 2 * h + 1, tg)], writes=[K("ps", sbk)])
            pt2 = PT2[i % 2]
            sch.add("act", (lambda pt2, sbk, mt: lambda e: e.activation(out=pt2[:, mt, :], in_=ps[:, sbk, :], func=AF.Exp, scale=1.0 / 16.0))(pt2, sbk, mt),
                    reads=[K("ps", sbk)], writes=[K("PT2", i % 2, mt)])

        pend_X = [None]
        emit_SC(0, 0)
        emit_SC(0, 1)
        for i, (tg, h) in enumerate(xsteps):
            if i + 1 < len(xsteps):
                emit_SC(i + 1, 0)
                emit_SC(i + 1, 1)
            pt2 = PT2[i % 2]
            xo_tm = xo_tms[tg % 2]
            xb = tg % 2

            def pv_half(hf, pt2=pt2, h=h, i=i):
                for qi in (2 * hf, 2 * hf + 1):
                    mm(ps[:, 4 + qi, 0:257], [(pt2[:, mt, qi * 128:(qi + 1) * 128], Vc[:, mt, h, 0:257]) for mt in range(2)],
                       reads=[K("PT2", i % 2, 0), K("PT2", i % 2, 1), K("Vc", 0, h // 2), K("Vc", 1, h // 2), K("Vc", "ones")], writes=[K("ps", 4 + qi)])

            def fin_half(hf, h=h, xo_tm=xo_tm, xb=xb):
                accs = [K("ps", 4 + 2 * hf), K("ps", 5 + 2 * hf)]
                rrh = rr2[:, 2 * hf:2 * hf + 2]
                sch.add("dve", (lambda rrh, hf: lambda e: e.reciprocal(out=rrh.unsqueeze(2), in_=ps[:, 4 + 2 * hf:6 + 2 * hf, 256:257]))(rrh, hf), reads=accs, writes=[K("rr2", hf)])
                r2bc = rrh.unsqueeze(2).to_broadcast([128, 2, 256])
                sch.add("dve", (lambda h, hf, r2bc: lambda e: e.tensor_tensor(out=xo_tm[:, 2 * hf:2 * hf + 2, h * 256:(h + 1) * 256], in0=ps[:, 4 + 2 * hf:6 + 2 * hf, 0:256], in1=r2bc, op=ALU.mult))(h, hf, r2bc),
                        reads=accs + [K("rr2", hf)], writes=[K("xo_tm", xb, qi, h) for qi in (2 * hf, 2 * hf + 1)])

            pv_half(0)
            fin_half(0)
            pv_half(1)
            if pend_X[0] is not None:
                pend_X[0]()
                pend_X[0] = None
            fin_half(1)
            if h == 3:
                def fin_X(tg=tg, xo_tm=xo_tm, xb=xb):
                    for qi in range(4):
                        tt = tg * 4 + qi
                        pb = psb(3)

                        def tfn3(e, pb=pb, qi=qi):
                            ins = None
                            for kc in range(KC):
                                ins = e.transpose(out=pb[:, kc, :], in_=xo_tm[:, qi, kc * 128:(kc + 1) * 128], identity=ident[:, :])
                            return ins
                        sch.add("pe", tfn3, reads=[K("xo_tm", xb, qi, hh) for hh in range(4)] + [K("ident")], writes=[K("ps", 3)])
                        evac(alt_eng(), actT[:, :, tt * 128:(tt + 1) * 128], pb[:, :, :], reads=[K("ps", 3)], writes=actT_keys(range(KC), [tt]))
                pend_X[0] = fin_X
        if upto == "2d" and pend_X[0] is not None:
            pend_X[0]()
            pend_X[0] = None
        if upto == "2d":
            return _finish(nc, es, sch, dma_slots, dbg_d, [(actT[:, :, :].rearrange("p a b -> p (a b)"), actT_keys(range(KC), range(NT)), KC * S, BF16)], locals())
        load_gb("g_x_post")
        pipe = Pipe(1)
        for tt in range(NT):
            if tt == 2 and pend_X[0] is not None:
                pend_X[0]()
                pend_X[0] = None
            b0 = next_pair()
            for cg in range(2):
                mm(ps[:, b0 + cg, :], [(actT[:, kc, tt * 128:(tt + 1) * 128], wsl2[cg][:, kc, :]) for kc in range(KC)],
                   reads=[K("wsl2", cg)] + actT_keys(range(KC), [tt]), writes=[K("ps", b0 + cg)])
            pipe.push(post_norm_res("p2", tt, b0, X[:, tt, :], [K("X", tt)], X[:, tt, :], [K("X", tt)], ytmp[tt % 2], K("ytmp2", tt % 2)))
        pipe.flush()
        if upto == "2e":
            return _finish(nc, es, sch, dma_slots, dbg_d, [(X[:, :, :].rearrange("p a b -> p (a b)"), [K("X", tt) for tt in range(NT)], NT * D, F32)], locals())

        R2.reset()
        wd = R2.alloc([NFC, D], BF16)
        hidT = R2.alloc([NFC, 512], BF16)
        gsl = [R2.alloc([KC, 256], BF16) for _ in range(2)]
        usl = [R2.alloc([KC, 256], BF16) for _ in range(2)]
        sgb = [R2.alloc([1, 512], F32)[:, 0, :] for _ in range(2)]
        ytmp3 = [R2.alloc([1, D], F32)[:, 0, :] for _ in range(2)]
        phase_barrier()
        load_gb("g_ffn_pre")
        gu_n = [0]

        def load_gu(q4, s):
            i = gu_n[0]
            gu_n[0] += 1
            sl = i % 2
            for (nm, bufs, scr, sk) in (("gsl", gsl, wg_s, "wgscr"), ("usl", usl, wu_s, "wuscr")):
                flat = bufs[sl][:, :, :].rearrange("p a b -> p (a b)")
                dma_load("sp", "%s%d" % (nm, sl), flat, scr[s, :, :], writes=[K(nm, sl)], reads=[K(sk, s)])

        gu_list = [(q4, s) for q4 in range(4) for s in range(11)]
        load_gu(*gu_list[0])
        load_gu(*gu_list[1])
        drain_conv(1000)
        dma_load("sp", "wd", wd[:, :, :].rearrange("p a b -> p (a b)"), wd_s[:, :], writes=[K("wd")], reads=[K("wdscr", hf) for hf in range(4)])
        for tt in range(NT):
            norm_stats(tt, X[:, tt, :], [K("X", tt)], htmp[:, tt % 2, :], K("htmp", tt % 2))
        xpipe3 = Pipe(1)
        x3 = [0]

        def x3_tile():
            if x3[0] < NT:
                tt = x3[0]
                x3[0] += 1
                bank_rr[0] = 4 + (tt % 4)
                xpipe3.push(norm_scale(tt, X[:, tt, :], [K("X", tt)], actT[:, :, tt * 128:(tt + 1) * 128], actT_keys(range(KC), [tt]), ev="act"))
                if x3[0] == NT:
                    bank_rr[0] = 4
                    xpipe3.flush()
                    load_gb("g_ffn_post")

        for _ in range(4):
            x3_tile()
        bank_rr[0] = 4
        xpipe3.flush()
        gi = 0
        stepc = 0
        pipe3 = Pipe(1)
        for q4 in range(4):
            tks = actT_keys(range(KC), range(4 * q4, 4 * q4 + 4))
            for s in range(11):
                sl = gi % 2
                if q4 == 0:
                    x3_tile()
                    if s == 0:
                        x3_tile()
                for f2 in range(2):
                    fc = 2 * s + f2
                    pr = (stepc % 2) * 2
                    stepc += 1
                    mm(ps[:, pr, :], [(gsl[sl][:, kc, f2 * 128:(f2 + 1) * 128], actT[:, kc, q4 * 512:(q4 + 1) * 512]) for kc in range(KC)],
                       reads=[K("gsl", sl)] + tks, writes=[K("ps", pr)])
                    mm(ps[:, pr + 1, :], [(usl[sl][:, kc, f2 * 128:(f2 + 1) * 128], actT[:, kc, q4 * 512:(q4 + 1) * 512]) for kc in range(KC)],
                       reads=[K("usl", sl)] + tks, writes=[K("ps", pr + 1)])
                    sg = sgb[stepc % 2]
                    sgk = K("sg", stepc % 2)
                    sch.add("act", (lambda sg, pr: lambda e: e.activation(out=sg, in_=ps[:, pr, :], func=AF.Silu))(sg, pr), reads=[K("ps", pr)], writes=[sgk])
                    sch.add("dve", (lambda sg, pr, fc: lambda e: e.tensor_tensor(out=hidT[:, fc, :], in0=sg, in1=ps[:, pr + 1, :], op=ALU.mult))(sg, pr, fc),
                            reads=[sgk, K("ps", pr + 1)], writes=[K("hidT", fc)])
                gi += 1
                if gi + 1 < len(gu_list):
                    load_gu(*gu_list[gi + 1])
            for tl in range(4):
                tt = q4 * 4 + tl
                b0 = 4 + (tl % 2) * 2
                for cg in range(2):
                    mm(ps[:, b0 + cg, :], [(hidT[:, fc, tl * 128:(tl + 1) * 128], wd[:, fc, cg * 512:(cg + 1) * 512]) for fc in range(NFC)],
                       reads=[K("wd")] + [K("hidT", fc) for fc in range(NFC)], writes=[K("ps", b0 + cg)])
                def store(tt=tt):
                    slot("ost%d" % (tt % 2))
                    sch.add("sp", lambda e: e.dma_start(out=out_d[tt * 128:(tt + 1) * 128, :], in_=X[:, tt, :]), reads=[K("X", tt)], writes=[K("outd", tt)], dma="ost%d" % (tt % 2))
                pipe3.push(post_norm_res("p3", tt, b0, X[:, tt, :], [K("X", tt)], X[:, tt, :], [K("X", tt)], ytmp3[tt % 2], K("ytmp3", tt % 2), after=store))
        pipe3.flush()
        sch.final_slots = ["ost0", "ost1"]
        return _finish(nc, es, sch, dma_slots, None, [], locals())


def _finish(nc, es, sch, dma_slots, dbg_d, dumps, env):
    if dbg_d is not None and dumps:
        r2_t = env["r2_t"]
        off = 0
        for i, (ap, keys, n, dt) in enumerate(dumps):
            words = n if dt == F32 else n // 2
            src = ap if dt == F32 else ap.bitcast(F32)
            name = "dbg%d" % i
            if name not in dma_slots:
                dma_slots.append(name)
            sch.add("sp", (lambda src, off, words: lambda e: e.dma_start(out=dbg_d[:, off:off + words], in_=src))(src, off, words), reads=keys, writes=[("dbgout", i)], dma=name)
            sch.final_slots.append(name)
            off += words
    eng_sem = {}
    dma_sem = {}
    for e in Sched.ENGS:
        eng_sem[e] = es.enter_context(nc.semaphore("s_" + e))
    for s in dma_slots:
        dma_sem[s] = es.enter_context(nc.semaphore("d_" + s))
    block = es.enter_context(nc.Block())
    sch.emit(nc, block, eng_sem, dma_sem)
    return nc


def make_in_maps(inputs):
    f = lambda a: np.ascontiguousarray(np.asarray(a), dtype=np.float32)
    x = f(inputs["x"])
    mem = f(inputs["mem"])
    pos = np.asarray(inputs["positions"]).astype(np.int32)
    lamv = np.zeros((1, 256), np.float32)
    lamv[0, 0:64] = f(inputs["lambda_q1"])
    lamv[0, 64:128] = f(inputs["lambda_k1"])
    lamv[0, 128:192] = f(inputs["lambda_q2"])
    lamv[0, 192:256] = f(inputs["lambda_k2"])
    shared = {
        "lamv": lamv,
        "g_subln": f(inputs["g_subln"]).reshape(1, 128),
        "w_pool": np.ascontiguousarray(f(inputs["w_pool"]).transpose(1, 0, 2)),
        "pool_scale": np.ascontiguousarray(f(inputs["pool_scale"]).reshape(4, 128).T),
    }
    for n in ("g_mix_pre", "g_mix_post", "g_x_pre", "g_mem", "g_x_post", "g_ffn_pre", "g_ffn_post"):
        shared[n] = f(inputs[n]).reshape(1, D)
    for n in ("w_in", "w_out", "w_xq", "w_xkv", "w_xo", "w_gate", "w_up", "w_down"):
        shared[n] = f(inputs[n])
    maps = []
    for b in range(8):
        m = dict(shared)
        m["x"] = x[b]
        m["mem"] = mem[b]
        m["pos"] = np.ascontiguousarray(pos[b].reshape(NT, 128).T)
        maps.append(m)
    return maps


def kernel(**inputs):
    nc = build_nc()
    maps = make_in_maps(inputs)
    res = run_bass_kernel_spmd(nc, maps, core_ids=list(range(8)))
    out = np.stack([np.asarray(r["out"], dtype=np.float32) for r in res.results], axis=0)
    return out
```
